# Optimizing a Trainium2 kernel written in Bass

```python
import math
import jax, jax.numpy as jnp
from jax import lax
import numpy as np

D_MODEL = 2048
BATCH = 4
SEQ = 2048
DEPTH = 4

GRID_W = 64
CTX_LEN = 256
HEAD_DIM = 128
N_GROUPS = 4
GROUP_HEADS = D_MODEL // (N_GROUPS * HEAD_DIM)
GROUP_WIDTH = GROUP_HEADS * HEAD_DIM
MIX_WIDTH = N_GROUPS * GROUP_WIDTH
KV_HEADS = 2
KV_WIDTH = KV_HEADS * HEAD_DIM
DIFF_DIM = HEAD_DIM // 2
WINDOW = 128
Q_BLOCK = 128
NA_WIN_H = 8
NA_WIN_W = 16
NA_COL_BLOCK = 16
NA_KEY_COLS = NA_COL_BLOCK + NA_WIN_W
ROPE_THETA = 10000.0
EPS = 1e-6
NEG_INF = -1e30

IN_SIZES = (GROUP_WIDTH, GROUP_WIDTH, GROUP_WIDTH,
            GROUP_WIDTH, KV_WIDTH, KV_WIDTH,
            GROUP_WIDTH, GROUP_WIDTH, GROUP_WIDTH,
            GROUP_WIDTH, KV_WIDTH, KV_WIDTH,
            MIX_WIDTH)
IN_WIDTH = sum(IN_SIZES)

kernel_name = 'hybrid_parallel_group_dit_block'


def rms_norm(x, g):
    xf = x.astype(jnp.float32)
    y = xf * lax.rsqrt(jnp.mean(xf * xf, axis=-1, keepdims=True) + EPS)
    return (y * g.astype(jnp.float32)).astype(x.dtype)


def axial_rope_tables(n_tokens, dim, dtype):
    t = jnp.arange(n_tokens)
    row = (t // GRID_W).astype(jnp.float32)
    col = (t % GRID_W).astype(jnp.float32)
    n_freq = dim // 4
    inv = ROPE_THETA ** (-jnp.arange(n_freq, dtype=jnp.float32) / n_freq)
    ang = jnp.concatenate([row[:, None] * inv, col[:, None] * inv], axis=-1)
    return jnp.cos(ang).astype(dtype), jnp.sin(ang).astype(dtype)


def apply_rope(x, cos, sin):
    h = x.shape[-1] // 2
    x1, x2 = x[..., :h], x[..., h:]
    return jnp.concatenate([x1 * cos - x2 * sin, x2 * cos + x1 * sin], axis=-1)


def to_heads(t, n_heads, dim):
    b, s, _ = t.shape
    return t.reshape(b, s, n_heads, dim).transpose(0, 2, 1, 3)


def from_heads(t):
    b, h, s, d = t.shape
    return t.transpose(0, 2, 1, 3).reshape(b, s, h * d)


def diff_attention(q_lat, k_lat, v_lat, q_ctx, k_ctx, v_ctx, lam, subln_g, lambda_init, rope, with_ctx):
    cos, sin = rope
    b, s, _ = q_lat.shape
    H = GROUP_HEADS
    nblk = s // Q_BLOCK
    scale = DIFF_DIM ** -0.5

    def split2(t):
        n = t.shape[1]
        return t.reshape(b, n, H, 2, DIFF_DIM).transpose(0, 2, 3, 1, 4)

    ql = apply_rope(split2(q_lat), cos, sin)
    kl = apply_rope(split2(k_lat), cos, sin)
    kc = split2(k_ctx)
    vl, vc = to_heads(v_lat, H, HEAD_DIM), to_heads(v_ctx, H, HEAD_DIM)
    k_all = jnp.concatenate([kc, kl], axis=3)
    v_all = jnp.concatenate([vc, vl], axis=2)

    def attend(q, k, v):
        p = jax.nn.softmax(jnp.einsum('bhpqd,bhpkd->bhpqk', q, k).astype(jnp.float32) * scale, axis=-1)
        a = p[:, :, 0] - lam * p[:, :, 1]
        return jnp.einsum('bhqk,bhkd->bhqd', a.astype(v.dtype), v)

    qb = jnp.moveaxis(ql.reshape(b, H, 2, nblk, Q_BLOCK, DIFF_DIM), 3, 0)
    ob = lax.map(lambda qq: attend(qq, k_all, v_all), qb)
    o_lat = jnp.moveaxis(ob, 0, 2).reshape(b, H, s, HEAD_DIM)

    def post(o):
        return from_heads(rms_norm(o, subln_g) * (1.0 - lambda_init))

    if not with_ctx:
        return post(o_lat), None
    return post(o_lat), post(attend(split2(q_ctx), kc, vc))


def _band(t, nblk, side):
    b, hk, s, d = t.shape
    tp = jnp.pad(t, ((0, 0), (0, 0), (side * Q_BLOCK, side * Q_BLOCK), (0, 0)))
    tp = tp.reshape(b, hk, nblk + 2 * side, Q_BLOCK, d)
    return jnp.concatenate([tp[:, :, j:j + nblk] for j in range(2 * side + 1)], axis=3)


def window_attention(q_lat, k_lat, v_lat, q_ctx, k_ctx, v_ctx, sink, rope, with_ctx):
    cos, sin = rope
    b, s, _ = q_lat.shape
    L = k_ctx.shape[1]
    g = GROUP_HEADS // KV_HEADS
    nblk = s // Q_BLOCK
    side = WINDOW // Q_BLOCK
    nk = (2 * side + 1) * Q_BLOCK
    scale = HEAD_DIM ** -0.5
    ql = apply_rope(to_heads(q_lat, GROUP_HEADS, HEAD_DIM), cos, sin).reshape(b, KV_HEADS, g, nblk, Q_BLOCK, HEAD_DIM)
    kl = apply_rope(to_heads(k_lat, KV_HEADS, HEAD_DIM), cos, sin)
    vl = to_heads(v_lat, KV_HEADS, HEAD_DIM)
    kc, vc = to_heads(k_ctx, KV_HEADS, HEAD_DIM), to_heads(v_ctx, KV_HEADS, HEAD_DIM)
    kb, vb = _band(kl, nblk, side), _band(vl, nblk, side)
    n = np.arange(nblk)[:, None, None]
    t = np.arange(Q_BLOCK)[None, :, None]
    j = np.arange(nk)[None, None, :]
    qpos = n * Q_BLOCK + t
    kpos = (n - side) * Q_BLOCK + j
    valid = (np.abs(kpos - qpos) <= WINDOW) & (kpos >= 0) & (kpos < s)
    sink_hg = sink.astype(jnp.float32).reshape(KV_HEADS, g)
    s_band = jnp.where(valid, jnp.einsum('bkgnqd,bknjd->bkgnqj', ql, kb).astype(jnp.float32) * scale, NEG_INF)
    s_ctx = jnp.einsum('bkgnqd,bkld->bkgnql', ql, kc).astype(jnp.float32) * scale
    s_sink = jnp.broadcast_to(sink_hg[None, :, :, None, None, None], s_ctx.shape[:-1] + (1,))
    p = jax.nn.softmax(jnp.concatenate([s_band, s_ctx, s_sink], axis=-1), axis=-1).astype(vl.dtype)
    o = (jnp.einsum('bkgnqj,bknjd->bkgnqd', p[..., :nk], vb)
         + jnp.einsum('bkgnql,bkld->bkgnqd', p[..., nk:nk + L], vc))
    o_lat = from_heads(o.reshape(b, GROUP_HEADS, s, HEAD_DIM))
    if not with_ctx:
        return o_lat, None
    qc = to_heads(q_ctx, GROUP_HEADS, HEAD_DIM).reshape(b, KV_HEADS, g, L, HEAD_DIM)
    sc = jnp.einsum('bkgqd,bkld->bkgql', qc, kc).astype(jnp.float32) * scale
    sink_c = jnp.broadcast_to(sink_hg[None, :, :, None, None], sc.shape[:-1] + (1,))
    pc = jax.nn.softmax(jnp.concatenate([sc, sink_c], axis=-1), axis=-1)[..., :L].astype(vc.dtype)
    oc = jnp.einsum('bkgql,bkld->bkgqd', pc, vc).reshape(b, GROUP_HEADS, L, HEAD_DIM)
    return o_lat, from_heads(oc)


def neighborhood_attention(q_lat, k_lat, v_lat, q_ctx, k_ctx, v_ctx, rpb, with_ctx):
    b, s, _ = q_lat.shape
    rows = s // GRID_W
    H = GROUP_HEADS
    scale = HEAD_DIM ** -0.5
    wr = min(NA_WIN_H, rows)
    n_cb = GRID_W // NA_COL_BLOCK
    K = wr * NA_KEY_COLS
    r = np.arange(rows)
    key_rows = np.clip(r - wr // 2, 0, rows - wr)[:, None] + np.arange(wr)
    cb = np.arange(n_cb)
    key_cols = (np.clip(cb * NA_COL_BLOCK - NA_WIN_W // 2, 0, GRID_W - NA_KEY_COLS)[:, None]
                + np.arange(NA_KEY_COLS))
    qcol = cb[:, None] * NA_COL_BLOCK + np.arange(NA_COL_BLOCK)
    cstart = np.clip(qcol - NA_WIN_W // 2, 0, GRID_W - NA_WIN_W)
    key_idx = (key_rows[:, None, :, None] * GRID_W + key_cols[None, :, None, :]).reshape(rows, n_cb, K)
    col_ok = (key_cols[:, None, :] >= cstart[..., None]) & (key_cols[:, None, :] < cstart[..., None] + NA_WIN_W)
    valid = np.broadcast_to(col_ok[:, :, None, :], (n_cb, NA_COL_BLOCK, wr, NA_KEY_COLS)).reshape(n_cb, NA_COL_BLOCK, K)
    rel_r = key_rows - r[:, None] + NA_WIN_H - 1
    rel_c = np.clip(key_cols[:, None, :] - qcol[..., None] + NA_WIN_W - 1, 0, 2 * NA_WIN_W - 2)
    bias = rpb.astype(jnp.float32)[:, rel_r[:, None, None, :, None], rel_c[None, :, :, None, :]]
    bias = jnp.where(valid, bias.reshape(H, rows, n_cb, NA_COL_BLOCK, K), NEG_INF)

    ql = to_heads(q_lat, H, HEAD_DIM).reshape(b, H, rows, n_cb, NA_COL_BLOCK, HEAD_DIM)
    kl, vl = to_heads(k_lat, H, HEAD_DIM), to_heads(v_lat, H, HEAD_DIM)
    kc, vc = to_heads(k_ctx, H, HEAD_DIM), to_heads(v_ctx, H, HEAD_DIM)
    kg, vg = kl[:, :, key_idx], vl[:, :, key_idx]
    s_nb = jnp.einsum('bhrcqd,bhrckd->bhrcqk', ql, kg).astype(jnp.float32) * scale + bias
    s_ctx = jnp.einsum('bhrcqd,bhld->bhrcql', ql, kc).astype(jnp.float32) * scale
    p = jax.nn.softmax(jnp.concatenate([s_nb, s_ctx], axis=-1), axis=-1).astype(vl.dtype)
    o = (jnp.einsum('bhrcqk,bhrckd->bhrcqd', p[..., :K], vg)
         + jnp.einsum('bhrcql,bhld->bhrcqd', p[..., K:], vc))
    o_lat = from_heads(o.reshape(b, H, s, HEAD_DIM))
    if not with_ctx:
        return o_lat, None
    qc = to_heads(q_ctx, H, HEAD_DIM)
    pc = jax.nn.softmax(jnp.einsum('bhqd,bhld->bhql', qc, kc).astype(jnp.float32) * scale, axis=-1).astype(vc.dtype)
    return o_lat, from_heads(jnp.einsum('bhql,bhld->bhqd', pc, vc))


def qknorm_attention(q_lat, k_lat, v_lat, q_ctx, k_ctx, v_ctx, gq, gk, rope, with_ctx):
    cos, sin = rope
    b, s, _ = q_lat.shape
    L = k_ctx.shape[1]
    g = GROUP_HEADS // KV_HEADS
    nblk = s // Q_BLOCK
    scale = HEAD_DIM ** -0.5
    ql = apply_rope(rms_norm(to_heads(q_lat, GROUP_HEADS, HEAD_DIM), gq), cos, sin).reshape(b, KV_HEADS, g, s, HEAD_DIM)
    kl = apply_rope(rms_norm(to_heads(k_lat, KV_HEADS, HEAD_DIM), gk), cos, sin)
    vl = to_heads(v_lat, KV_HEADS, HEAD_DIM)
    kc = rms_norm(to_heads(k_ctx, KV_HEADS, HEAD_DIM), gk)
    vc = to_heads(v_ctx, KV_HEADS, HEAD_DIM)
    k_all = jnp.concatenate([kc, kl], axis=2)
    v_all = jnp.concatenate([vc, vl], axis=2)

    def attend(q, k, v):
        p = jax.nn.softmax(jnp.einsum('bkgqd,bksd->bkgqs', q, k).astype(jnp.float32) * scale, axis=-1)
        return jnp.einsum('bkgqs,bksd->bkgqd', p.astype(v.dtype), v)

    qb = jnp.moveaxis(ql.reshape(b, KV_HEADS, g, nblk, Q_BLOCK, HEAD_DIM), 3, 0)
    ob = lax.map(lambda qq: attend(qq, k_all, v_all), qb)
    o_lat = from_heads(jnp.moveaxis(ob, 0, 3).reshape(b, GROUP_HEADS, s, HEAD_DIM))
    if not with_ctx:
        return o_lat, None
    qc = rms_norm(to_heads(q_ctx, GROUP_HEADS, HEAD_DIM), gq).reshape(b, KV_HEADS, g, L, HEAD_DIM)
    return o_lat, from_heads(attend(qc, kc, vc).reshape(b, GROUP_HEADS, L, HEAD_DIM))


def setup_inputs(seed: int = 0) -> dict:
    key = jax.random.key(seed)
    ks = jax.random.split(key, 20)
    f32 = jnp.float32
    nrm = lambda k, shape: jax.random.normal(k, shape, f32)
    return {
        'x': nrm(ks[0], (BATCH, SEQ, D_MODEL)),
        'c': nrm(ks[1], (BATCH, D_MODEL)),
        'ctx': nrm(ks[2], (BATCH, CTX_LEN, D_MODEL)),
        'c_ctx': nrm(ks[3], (D_MODEL,)),
        'w_mod': nrm(ks[4], (DEPTH, D_MODEL, 3 * D_MODEL)) * (0.5 * D_MODEL ** -0.5),
        'b_mod': nrm(ks[5], (DEPTH, 3 * D_MODEL)) * 0.02,
        'pre_norm_g': 1.0 + 0.02 * nrm(ks[6], (DEPTH, D_MODEL)),
        'w_in': nrm(ks[7], (DEPTH, D_MODEL, IN_WIDTH)) * D_MODEL ** -0.5,
        'diff_lambda_q1': nrm(ks[8], (DEPTH, DIFF_DIM)) * 0.1,
        'diff_lambda_k1': nrm(ks[9], (DEPTH, DIFF_DIM)) * 0.1,
        'diff_lambda_q2': nrm(ks[10], (DEPTH, DIFF_DIM)) * 0.1,
        'diff_lambda_k2': nrm(ks[11], (DEPTH, DIFF_DIM)) * 0.1,
        'diff_subln_g': 1.0 + 0.02 * nrm(ks[12], (DEPTH, HEAD_DIM)),
        'win_sink': nrm(ks[13], (DEPTH, GROUP_HEADS)) * 0.5,
        'na_rpb': nrm(ks[14], (DEPTH, GROUP_HEADS, 2 * NA_WIN_H - 1, 2 * NA_WIN_W - 1)) * 0.1,
        'qk_q_norm_g': 1.0 + 0.02 * nrm(ks[15], (DEPTH, HEAD_DIM)),
        'qk_k_norm_g': 1.0 + 0.02 * nrm(ks[16], (DEPTH, HEAD_DIM)),
        'w_out': nrm(ks[17], (DEPTH, MIX_WIDTH, D_MODEL)) * MIX_WIDTH ** -0.5,
        'post_norm_g': 1.0 + 0.02 * nrm(ks[18], (DEPTH, D_MODEL)),
    }


def reference(x, c, ctx, c_ctx, w_mod, b_mod, pre_norm_g, w_in, diff_lambda_q1, diff_lambda_k1,
              diff_lambda_q2, diff_lambda_k2, diff_subln_g, win_sink, na_rpb, qk_q_norm_g, qk_k_norm_g,
              w_out, post_norm_g):
    b, s, _ = x.shape
    rope_sub = axial_rope_tables(s, DIFF_DIM, x.dtype)
    rope_head = axial_rope_tables(s, HEAD_DIM, x.dtype)
    split_points = [int(v) for v in np.cumsum(IN_SIZES)[:-1]]
    silu_c = jax.nn.silu(c)
    silu_cc = jax.nn.silu(c_ctx)
    h, hc = x, ctx
    for l in range(DEPTH):
        with_ctx = l < DEPTH - 1
        shift, scale, gate = jnp.split(silu_c @ w_mod[l] + b_mod[l], 3, axis=-1)
        shift_c, scale_c, gate_c = jnp.split(silu_cc @ w_mod[l] + b_mod[l], 3, axis=-1)
        n = rms_norm(h, pre_norm_g[l]) * (1.0 + scale[:, None]) + shift[:, None]
        nc = rms_norm(hc, pre_norm_g[l]) * (1.0 + scale_c) + shift_c
        p = jnp.split(n @ w_in[l], split_points, axis=-1)
        pc = jnp.split(nc @ w_in[l], split_points, axis=-1)

        lambda_init = 0.8 - 0.6 * math.exp(-0.3 * l)
        lam = (jnp.exp(jnp.sum(diff_lambda_q1[l].astype(jnp.float32) * diff_lambda_k1[l].astype(jnp.float32)))
               - jnp.exp(jnp.sum(diff_lambda_q2[l].astype(jnp.float32) * diff_lambda_k2[l].astype(jnp.float32)))
               + lambda_init)
        oa, oa_c = diff_attention(p[0], p[1], p[2], pc[0], pc[1], pc[2], lam, diff_subln_g[l], lambda_init,
                                  rope_sub, with_ctx)
        ob, ob_c = window_attention(p[3], p[4], p[5], pc[3], pc[4], pc[5], win_sink[l], rope_head, with_ctx)
        on, on_c = neighborhood_attention(p[6], p[7], p[8], pc[6], pc[7], pc[8], na_rpb[l], with_ctx)
        od, od_c = qknorm_attention(p[9], p[10], p[11], pc[9], pc[10], pc[11], qk_q_norm_g[l], qk_k_norm_g[l],
                                    rope_head, with_ctx)

        y = (jnp.concatenate([oa, ob, on, od], axis=-1) * jax.nn.silu(p[12])) @ w_out[l]
        h = h + gate[:, None] * rms_norm(y, post_norm_g[l])
        if with_ctx:
            yc = (jnp.concatenate([oa_c, ob_c, on_c, od_c], axis=-1) * jax.nn.silu(pc[12])) @ w_out[l]
            hc = hc + gate_c * rms_norm(yc, post_norm_g[l])
    return h
```

```python
import math
import contextlib
import numpy as np
import ml_dtypes
import concourse.bass as bass
import concourse.mybir as mybir
from concourse.bass_utils import run_bass_kernel_spmd

F32 = mybir.dt.float32
BF16 = mybir.dt.bfloat16
AF = mybir.ActivationFunctionType
ALU = mybir.AluOpType
AX = mybir.AxisListType

NEG = -30000.0
EPS = 1e-6
D = 2048
NLAT = 1024
NCTX = 256
NTOK = NLAT + NCTX
GRID_W = 64
NCT = 27

C_DELTAS = {0: [-2, -1, 0, 1, 2, 3], 1: [-2, -1, 0, 1, 2], 6: [-2, -1, 0, 1, 2], 7: [-3, -2, -1, 0, 1, 2]}
for _j in range(2, 6):
    C_DELTAS[_j] = [-2, -1, 0, 1, 2]
C_TAB0 = {0: 0, 1: 6, 2: 11, 3: 11, 4: 11, 5: 11, 6: 16, 7: 21}


class Trk:
    def __init__(self, nc, st, ndma=40):
        self.nc = nc
        self.E = {'pe': nc.tensor, 'act': nc.scalar, 'dve': nc.vector, 'pool': nc.gpsimd, 'sp': nc.sync}
        self.sems = {}
        for k in ['pe', 'act', 'dve', 'pool']:
            self.sems[k] = st.enter_context(nc.semaphore("s_" + k))
        for i in range(ndma):
            self.sems[('d', i)] = st.enter_context(nc.semaphore("s_d%d" % i))
        self.sems['cc'] = st.enter_context(nc.semaphore("s_cc"))
        self.cnt = {k: 0 for k in self.sems}
        self.known = {e: {} for e in self.E}
        self.ndma = ndma
        self.dnext = 0
        self.res = {}
        self.retired = {}
        self.persist = set()
        self.nwait = 0
        self.log = {e: [] for e in self.E}

    def _st(self, key):
        s = self.res.get(key)
        if s is None:
            s = [None, dict(self.retired)]
            self.res[key] = s
        return s

    def _deps(self, reads, writes):
        need = {}

        def add(tok):
            if tok is None:
                return
            k, v = tok
            if v > need.get(k, 0):
                need[k] = v
        for r in reads:
            add(self._st(r)[0])
        for w in writes:
            s = self._st(w)
            add(s[0])
            for k, v in s[1].items():
                add((k, v))
        return need

    def _wait(self, e, need):
        kn = self.known[e]
        for k, v in need.items():
            if kn.get(k, 0) >= v:
                continue
            self.E[e].wait_ge(self.sems[k], v)
            self.log[e].append(('wait', k, v))
            kn[k] = v
            self.nwait += 1

    def _commit(self, tok, reads, writes):
        k, v = tok
        for r in reads:
            s = self._st(r)
            if v > s[1].get(k, 0):
                s[1][k] = v
        for w in writes:
            self.res[w] = [tok, {}]

    def op(self, e, fn, reads=(), writes=()):
        need = self._deps(reads, writes)
        if e == 'pe':
            need.pop('pe', None)
        self._wait(e, need)
        ins = fn(self.E[e])
        self.cnt[e] += 1
        ins.then_inc(self.sems[e], 1)
        self.log[e].append(('inc', e, 1))
        tok = (e, self.cnt[e])
        self._commit(tok, reads, writes)
        return tok

    def dma(self, q, out, in_, reads=(), writes=(), **kw):
        need = self._deps(reads, writes)
        i = self.dnext
        self.dnext = (i + 1) % self.ndma
        key = ('d', i)
        if self.cnt[key] > 0 and 16 * self.cnt[key] > need.get(key, 0):
            need[key] = 16 * self.cnt[key]
        self._wait(q, need)
        self.E[q].dma_start(out=out, in_=in_, **kw).then_inc(self.sems[key], 16)
        self.log[q].append(('inc', key, 16))
        self.cnt[key] += 1
        tok = (key, 16 * self.cnt[key])
        self._commit(tok, reads, writes)
        return tok

    def allgather(self, in_ap, out_ap, reads, writes):
        self._wait('pool', self._deps(reads, writes))
        self.nc.gpsimd.collective_compute(
            "AllGather", ALU.bypass, replica_groups=[[0, 1], [2, 3], [4, 5], [6, 7]],
            ins=[in_ap.opt()], outs=[out_ap.opt()]).then_inc(self.sems['cc'])
        self.log['pool'].append(('inc', 'cc', 1))
        self.cnt['cc'] += 1
        tok = ('cc', self.cnt['cc'])
        self._commit(tok, reads, writes)
        return tok

    def retire(self):
        for key in list(self.res.keys()):
            if key[0] in self.persist:
                continue
            s = self.res.pop(key)
            toks = dict(s[1])
            if s[0] is not None:
                k, v = s[0]
                if v > toks.get(k, 0):
                    toks[k] = v
            for k, v in toks.items():
                if v > self.retired.get(k, 0):
                    self.retired[k] = v

    def finish(self, e='sp'):
        need = {}
        for k, c in self.cnt.items():
            if c > 0:
                need[k] = c * (16 if isinstance(k, tuple) else 1)
        self._wait(e, need)


def _bcast_rows(ap2d, nparts):
    return bass.AP(tensor=ap2d.tensor, offset=ap2d.offset, ap=[[0, nparts], [1, ap2d.shape[-1]]])


def build(layer_ids, write_ctx):
    NL = len(layer_ids)
    nc = bass.Bass("TRN2", target_bir_lowering=False)

    def din(name, shape, dt=F32):
        return nc.dram_tensor(name, shape, dt, kind="ExternalInput").ap()

    x_sh = din("x_sh", [NLAT, D])
    ctx_b = din("ctx_b", [NCTX, D])
    cvec = din("cvec", [2, D])
    w_mod = din("w_mod", [NL, D, 3 * D])
    b_mod = din("b_mod", [NL, 3 * D])
    pre_g = din("pre_g", [NL, D])
    post_g = din("post_g", [NL, D])
    w_in = din("w_in", [NL, D, 7168])
    w_out = din("w_out", [NL, D, D])
    lamv = din("lamv", [NL, 256])
    subln_g = din("subln_g", [NL, 128])
    sink = din("sink", [NL, 4])
    gq = din("gq", [NL, 128])
    gk = din("gk", [NL, 128])
    rope_t = din("rope_t", [NLAT, 384])
    ident_b = din("ident_b", [128, 128], BF16)
    ident_f = din("ident_f", [128, 128])
    ctab = din("ctab", [NL, 4, NCT, 128, 128])
    btab = din("btab", [6, 128, 128])
    sel_in = din("sel_in", [2, 256])
    out_sh = nc.dram_tensor("out_sh", [NLAT, D], F32, kind="ExternalOutput").ap()
    if write_ctx:
        ctx_out = nc.dram_tensor("ctx_out", [NCTX, D], F32, kind="ExternalOutput").ap()

    QT_d = nc.dram_tensor("QT_d", [16 * 128, NTOK], BF16).ap()
    GT_d = nc.dram_tensor("GT_d", [16 * 128, NTOK], BF16).ap()
    KTl_d = [nc.dram_tensor("KTl_d%d" % i, [6 * 128, NLAT], BF16).ap() for i in range(2)]
    KTc_d = nc.dram_tensor("KTc_d", [12 * 128, NCTX], BF16).ap()
    KTa_d = [nc.dram_tensor("KTa_d%d" % i, [2 * 6 * 128, NLAT], BF16).ap() for i in range(2)]
    Vl_d = [nc.dram_tensor("Vl_d%d" % i, [NLAT, 768], BF16).ap() for i in range(2)]
    Vc_d = nc.dram_tensor("Vc_d", [NCTX, 1536], BF16).ap()
    Va_d = [nc.dram_tensor("Va_d%d" % i, [2 * NLAT, 768], BF16).ap() for i in range(2)]
    gate_d = nc.dram_tensor("gate_d", [2, D], F32).ap()

    with contextlib.ExitStack() as st:
        T = Trk(nc, st)
        T.persist = {'h', 'nT', 'identb', 'identf', 'scT', 'gsT', 'shT', 'ceps',
                     'QT_d', 'GT_d', 'KTl_d', 'KTc_d', 'KTa_d', 'Vl_d', 'Vc_d', 'Va_d', 'gate_d', 'out'}

        uniq = [0]

        def sb(name, shape, dt, stack=st):
            uniq[0] += 1
            return stack.enter_context(nc.sbuf_tensor("%s_%d" % (name, uniq[0]), shape, dt))

        def ps(name, shape, dt, stack):
            uniq[0] += 1
            return stack.enter_context(nc.psum_tensor("%s_%d" % (name, uniq[0]), shape, dt))

        h = sb("h", [128, 8, D], F32)
        hc = sb("hc", [128, 2, D], F32)
        nT = sb("nT", [128, 16, NTOK], BF16)
        identb = sb("identb", [128, 128], BF16)
        identf = sb("identf", [128, 128], F32)
        scT = sb("scT", [128, 16, 2], BF16)
        gsT = sb("gsT", [128, 16, 2], F32)
        shT = sb("shT", [128, 16, 2], F32)
        ceps = sb("ceps", [128, 1], F32)

        def hview(tt):
            return h[:, tt, :] if tt < 8 else hc[:, tt - 8, :]

        for j in range(8):
            T.dma('sp', h[:, j, :], x_sh[j * 128:(j + 1) * 128, :], writes=[('h', j)])
        for j in range(2):
            T.dma('sp', hc[:, j, :], ctx_b[j * 128:(j + 1) * 128, :], writes=[('h', 8 + j)])
        T.dma('sp', identb[:], ident_b, writes=[('identb',)])
        T.dma('sp', identf[:], ident_f, writes=[('identf',)])
        T.op('dve', lambda e: e.memset(ceps[:], EPS), writes=[('ceps',)])
        with contextlib.ExitStack() as ph:
            cT = sb("cT", [128, 16, 2], F32, ph)
            for r in range(2):
                T.dma('sp', cT[:, :, r], cvec[r:r + 1, :].rearrange("o (k p) -> p (o k)", p=128),
                      writes=[('cT', r)], allow_slow_non_contiguous=True)
            T.op('act', lambda e: e.activation(out=scT[:], in_=cT[:], func=AF.Silu),
                 reads=[('cT', 0), ('cT', 1)], writes=[('scT',)])
        T.retire()

        def mm_group(out_ps, pairs, reads, writes):
            def fn(pe):
                n = len(pairs)
                ins = None
                for i, (l, r) in enumerate(pairs):
                    ins = pe.matmul(out_ps, l, r, start=(i == 0), stop=(i == n - 1))
                return ins
            return T.op('pe', fn, reads, writes)

        def mod_phase(li):
            with contextlib.ExitStack() as ph:
                wb = [sb("mwb%d" % i, [128, 16, 512], BF16, ph) for i in range(2)]
                bm = [sb("bm%d" % i, [2, 512], F32, ph) for i in range(2)]
                modrow = sb("modrow", [2, 3 * D], F32, ph)
                pregT = sb("pregT", [128, 16], F32, ph)
                tmpT = sb("tmpT", [128, 16, 2], F32, ph)
                pm = [ps("pm%d" % i, [128, 512], F32, ph) for i in range(2)]
                ptm = ps("ptm", [128, 512], F32, ph)
                T.dma('sp', pregT[:], pre_g[li:li + 1, :].rearrange("o (k p) -> p (o k)", p=128),
                      writes=[('pregT',)], allow_slow_non_contiguous=True)
                for cg in range(8):
                    b = cg % 2
                    cs = slice(cg * 512, (cg + 1) * 512)
                    T.dma('pool', wb[b][:], w_mod[li, :, cs].rearrange("(k p) c -> p k c", p=128),
                          writes=[('mwb', b)])
                    T.dma('sp', bm[b][:], _bcast_rows(b_mod[li:li + 1, cs], 2), writes=[('bm', b)])
                    mm_group(pm[b][0:2, :], [(scT[:, k, :], wb[b][:, k, :]) for k in range(16)],
                             reads=[('mwb', b), ('scT',)], writes=[('pm', b)])
                    T.op('dve', lambda e: e.tensor_tensor(out=modrow[0:2, cs], in0=pm[b][0:2, :],
                                                          in1=bm[b][0:2, :], op=ALU.add),
                         reads=[('pm', b), ('bm', b)], writes=[('modrow', cg)])

                def tr(pe):
                    ins = None
                    for part in range(2):
                        for k in range(16):
                            c = part * 16 + k
                            col = part * D + k * 128
                            ins = pe.transpose(ptm[:, c * 2:(c + 1) * 2], modrow[0:2, col:col + 128],
                                               identf[0:2, 0:2])
                    return ins
                T.op('pe', tr, reads=[('modrow', c) for c in range(8)] + [('identf',)], writes=[('ptm',)])
                T.op('dve', lambda e: e.tensor_copy(out=shT[:], in_=ptm[:, 0:32].rearrange("p (k r) -> p k r", r=2)),
                     reads=[('ptm',)], writes=[('shT',)])
                T.op('dve', lambda e: e.tensor_scalar(out=tmpT[:], in0=ptm[:, 32:64].rearrange("p (k r) -> p k r", r=2),
                                                      scalar1=1.0, scalar2=None, op0=ALU.add),
                     reads=[('ptm',)], writes=[('tmpT',)])
                T.op('dve', lambda e: e.tensor_tensor(out=gsT[:], in0=tmpT[:],
                                                      in1=pregT[:].unsqueeze(2).to_broadcast([128, 16, 2]),
                                                      op=ALU.mult),
                     reads=[('tmpT',), ('pregT',)], writes=[('gsT',)])
            T.retire()

        def mod_norm_phase(li):
            with contextlib.ExitStack() as ph:
                wb = [sb("mwb%d" % i, [128, 16, 512], BF16, ph) for i in range(2)]
                bm = [sb("mbm%d" % i, [2, 512], F32, ph) for i in range(2)]
                grow = [sb("grow%d" % i, [2, 512], F32, ph) for i in range(2)]
                pregT = sb("pregT", [128, 16], F32, ph)
                tmpT = sb("tmpT", [128, 16, 2], F32, ph)
                pm = [ps("mpm%d" % i, [128, 512], F32, ph) for i in range(2)]
                ptm = ps("ptm", [128, 512], F32, ph)
                xh = [sb("xh%d" % i, [128, D], BF16, ph) for i in range(2)]
                junk = sb("junk", [128, D], BF16, ph)
                stt = sb("nstat", [128, 10, 4], F32, ph)
                pt = [ps("pt%d" % i, [128, 1024], BF16, ph) for i in range(4)]
                NG = 12
                T.dma('sp', pregT[:], pre_g[li:li + 1, :].rearrange("o (k p) -> p (o k)", p=128),
                      writes=[('pregT',)], allow_slow_non_contiguous=True)

                def load(g):
                    b = g % 2
                    cs = slice(g * 512, (g + 1) * 512)
                    T.dma('pool', wb[b][:], w_mod[li, :, cs].rearrange("(k p) c -> p k c", p=128), writes=[('mwb', b)])
                    T.dma('sp', bm[b][:], _bcast_rows(b_mod[li:li + 1, cs], 2), writes=[('mbm', b)])

                def mstep(g):
                    b = g % 2
                    mm_group(pm[b][0:2, :], [(scT[:, k, :], wb[b][:, k, :]) for k in range(16)],
                             reads=[('mwb', b), ('scT',)], writes=[('mpm', b)])
                    T.op('dve', lambda e: e.tensor_tensor(out=grow[b][0:2, :], in0=pm[b][0:2, :], in1=bm[b][0:2, :], op=ALU.add),
                         reads=[('mpm', b), ('mbm', b)], writes=[('grow', b)])
                    if g < 8:
                        part = g // 4

                        def tr(pe):
                            ins = None
                            for kk in range(4):
                                c = part * 16 + (g % 4) * 4 + kk
                                ins = pe.transpose(ptm[:, c * 2:(c + 1) * 2], grow[b][0:2, kk * 128:(kk + 1) * 128],
                                                   identf[0:2, 0:2])
                            return ins
                        T.op('pe', tr, reads=[('grow', b), ('identf',)], writes=[('ptm', g)])
                    else:
                        T.dma('sp', gate_d[0:2, (g - 8) * 512:(g - 7) * 512], grow[b][0:2, :], reads=[('grow', b)],
                              writes=[('gate_d', g - 8)])
                    if g + 2 < NG:
                        load(g + 2)
                    if g == 7:
                        T.op('dve', lambda e: e.tensor_copy(out=shT[:], in_=ptm[:, 0:32].rearrange("p (k r) -> p k r", r=2)),
                             reads=[('ptm', c) for c in range(8)], writes=[('shT',)])
                        T.op('dve', lambda e: e.tensor_scalar(out=tmpT[:], in0=ptm[:, 32:64].rearrange("p (k r) -> p k r", r=2),
                                                              scalar1=1.0, scalar2=None, op0=ALU.add),
                             reads=[('ptm', c) for c in range(8)], writes=[('tmpT',)])
                        T.op('dve', lambda e: e.tensor_tensor(out=gsT[:], in0=tmpT[:],
                                                              in1=pregT[:].unsqueeze(2).to_broadcast([128, 16, 2]), op=ALU.mult),
                             reads=[('tmpT',), ('pregT',)], writes=[('gsT',)])
                load(0)
                load(1)
                msteps = [(lambda g=g: mstep(g)) for g in range(NG)]
                per_tile = [2, 1, 1, 1, 1, 1, 2, 1, 1, 1]
                for tt in range(10):
                    for _ in range(per_tile[tt]):
                        msteps.pop(0)()
                    hv = hview(tt)
                    xb_ = xh[tt % 2]
                    T.op('act', lambda e: e.activation(out=junk[:], in_=hv, func=AF.Square, accum_out=stt[:, tt, 0:1]),
                         reads=[('h', tt)], writes=[('nstat', tt, 0)])
                    T.op('act', lambda e: e.activation(out=stt[:, tt, 1:2], in_=stt[:, tt, 0:1], func=AF.Sqrt,
                                                       bias=ceps[:, 0:1], scale=1.0 / D),
                         reads=[('nstat', tt, 0), ('ceps',)], writes=[('nstat', tt, 1)])
                    T.op('dve', lambda e: e.reciprocal(out=stt[:, tt, 2:3], in_=stt[:, tt, 1:2]),
                         reads=[('nstat', tt, 1)], writes=[('nstat', tt, 2)])
                    T.op('dve', lambda e: e.tensor_scalar(out=xb_[:], in0=hv, scalar1=stt[:, tt, 2:3], scalar2=None,
                                                          op0=ALU.mult),
                         reads=[('h', tt), ('nstat', tt, 2)], writes=[('xh', tt % 2)])
                    for g in range(4):
                        ptv = pt[g][:, 0:512].rearrange("p (a b) -> p a b", b=128)

                        def trf(pe):
                            ins = None
                            for kk in range(4):
                                k = g * 4 + kk
                                ins = pe.transpose(ptv[:, kk, :], xb_[:, k * 128:(k + 1) * 128], identb[:])
                            return ins
                        T.op('pe', trf, reads=[('xh', tt % 2), ('identb',)], writes=[('pt', g)])
                        dst = nT[:, g * 4:(g + 1) * 4, tt * 128:(tt + 1) * 128]
                        wr = [('nT', tt, g * 4 + kk) for kk in range(4)]
                        if g % 2 == 0:
                            T.op('act', lambda e: e.activation(out=dst, in_=ptv, func=AF.Copy), reads=[('pt', g)], writes=wr)
                        else:
                            T.op('dve', lambda e: e.tensor_copy(out=dst, in_=ptv), reads=[('pt', g)], writes=wr)
                while msteps:
                    msteps.pop(0)()
                for k in range(16):
                    lat = nT[:, k, 0:NLAT]
                    T.op('dve', lambda e: e.tensor_scalar(out=lat, in0=lat, scalar1=gsT[:, k, 0:1], scalar2=shT[:, k, 0:1],
                                                          op0=ALU.mult, op1=ALU.add),
                         reads=[('gsT',), ('shT',)] + [('nT', tt, k) for tt in range(8)],
                         writes=[('nT', tt, k) for tt in range(8)])
                    cx = nT[:, k, NLAT:NTOK]
                    T.op('pool', lambda e: e.tensor_scalar(out=cx, in0=cx, scalar1=gsT[:, k, 1:2], scalar2=shT[:, k, 1:2],
                                                           op0=ALU.mult, op1=ALU.add),
                         reads=[('gsT',), ('shT',)] + [('nT', tt, k) for tt in (8, 9)],
                         writes=[('nT', tt, k) for tt in (8, 9)])
            T.retire()

        CGS = [
            [('T', 0, 512, 'Q', 0, 'rope_s')],
            [('T', 0, 512, 'K', 0, 'rope_s')],
            [('R', 0, 512, 'V', 0, None)],
            [('T', 0, 512, 'Q', 4, 'rope_h')],
            [('T', 0, 256, 'K', 4, 'rope_h'), ('R', 256, 256, 'V', 512, None)],
            [('T', 0, 512, 'Q', 8, 'plain')],
            [('T', 0, 512, 'K', 6, 'plain')],
            [('R', 0, 512, 'V', 768, None)],
            [('T', 0, 512, 'Q', 12, 'norm_q')],
            [('T', 0, 256, 'K', 10, 'norm_k'), ('R', 256, 256, 'V', 1280, None)],
            [('T', 0, 512, 'G', 0, 'silu')],
            [('T', 0, 512, 'G', 4, 'silu')],
            [('T', 0, 512, 'G', 8, 'silu')],
            [('T', 0, 512, 'G', 12, 'silu')],
        ]

        def exchange():
            for pi, kb_ in enumerate([(0, 4), (6, 10)]):
                T.allgather(KTl_d[pi], KTa_d[pi],
                            reads=[('KTl_d', b_, tp) for b_ in kb_ for tp in range(4)],
                            writes=[('KTa_d', pi)])
            for pi, vb_ in enumerate([(0, 512), (768, 1280)]):
                T.allgather(Vl_d[pi], Va_d[pi],
                            reads=[('Vl_d', b_, tp) for b_ in vb_ for tp in range(4)],
                            writes=[('Va_d', pi)])

        CG_ORDER = [1, 2, 4, 6, 7, 9, 0, 3, 5, 8, 10, 11, 12, 13]

        def proj_load_w(li, wb, ci_):
            cg_ = CG_ORDER[ci_]
            T.dma('pool', wb[ci_ % 2][:], w_in[li, :, cg_ * 512:(cg_ + 1) * 512].rearrange("(k p) c -> p k c", p=128),
                  writes=[('pwb', ci_ % 2)])

        def proj_phase(li, wb, with_ctx):
            with contextlib.ExitStack() as ph:
                xs = [sb("xs%d" % i, [128, 512], F32, ph) for i in range(2)]
                tu = [sb("tu%d" % i, [128, 512], F32, ph) for i in range(2)]
                NXB, NSTG, LAG = 6, 4, 3
                xb = [sb("xb%d" % i, [128, 512], BF16, ph) for i in range(NXB)]
                stgT = [sb("stgT%d" % i, [128, 4, 256], BF16, ph) for i in range(NSTG)]
                stgR = [sb("stgR%d" % i, [128, 2, 512], BF16, ph) for i in range(3)]
                gqb = sb("gqb", [128, 128], F32, ph)
                gkb = sb("gkb", [128, 128], F32, ph)
                rope = sb("rope", [128, 8, 384], F32, ph)
                nst = sb("pnst", [128, 8, 4], F32, ph)
                pj = [ps("pj%d" % i, [128, 512], F32, ph) for i in range(3)]
                ptq = [ps("ptq%d" % i, [128, 1024], BF16, ph) for i in range(2)]
                T.dma('sp', gqb[:], _bcast_rows(gq[li:li + 1, :], 128), writes=[('gqb',)])
                T.dma('sp', gkb[:], _bcast_rows(gk[li:li + 1, :], 128), writes=[('gkb',)])
                for j in range(8):
                    T.dma('sp', rope[:, j, :], rope_t[j * 128:(j + 1) * 128, :], writes=[('rope', j)])
                cnt = {'pj': 0, 'xs': 0, 'xb': 0, 'ptq': 0, 'T': 0, 'R': 0, 'nst': 0, 'n': 0}
                dq = []

                def run_deferred(now, lag):
                    while dq and dq[0][0] + lag <= now:
                        dq.pop(0)[1]()

                def emit_transposes(item):
                    (xbi, nh, sT, tt) = item
                    pi = cnt['ptq'] % 2
                    cnt['ptq'] += 1
                    pv_ = ptq[pi][:, 0:512].rearrange("p (a b) -> p a b", b=128)

                    def trf(pe):
                        ins = None
                        for hh in range(nh):
                            ins = pe.transpose(pv_[:, hh, :], xb[xbi][:, hh * 128:(hh + 1) * 128], identb[:])
                        return ins
                    T.op('pe', trf, reads=[('xb', xbi), ('identb',)], writes=[('ptq', pi)])
                    T.op('act', lambda e: e.activation(out=stgT[sT][:, 0:nh, (tt % 2) * 128:(tt % 2) * 128 + 128],
                                                       in_=pv_[:, 0:nh, :], func=AF.Copy),
                         reads=[('ptq', pi)], writes=[('stgT', sT, tt % 2)])

                def flush_T(seg, sT, tp):
                    (_, c0, W, dest, base, proc) = seg
                    nh = W // 128
                    src = stgT[sT][:, 0:nh, :]
                    rd = [('stgT', sT, 0), ('stgT', sT, 1)]
                    if dest == 'Q' or dest == 'G':
                        dd = QT_d if dest == 'Q' else GT_d
                        key = ('QT_d' if dest == 'Q' else 'GT_d', base, tp)
                        T.dma('sp', dd[base * 128:(base + nh) * 128, tp * 256:(tp + 1) * 256]
                              .rearrange("(hh d) t -> d hh t", d=128), src, reads=rd, writes=[key])
                    else:
                        if tp < 4:
                            T.dma('sp', KTl_d[base // 6][(base % 6) * 128:(base % 6 + nh) * 128, tp * 256:(tp + 1) * 256]
                                  .rearrange("(hh d) t -> d hh t", d=128), src, reads=rd,
                                  writes=[('KTl_d', base, tp)])
                        else:
                            T.dma('sp', KTc_d[base * 128:(base + nh) * 128, :]
                                  .rearrange("(hh d) t -> d hh t", d=128), src, reads=rd,
                                  writes=[('KTc_d', base)])

                def flush_R(seg, sR, tp):
                    (_, c0, W, dest, base, proc) = seg
                    src = stgR[sR][:, :, 0:W]
                    rd = [('stgR', sR, 0), ('stgR', sR, 1)]
                    if tp < 4:
                        T.dma('sp', Vl_d[base // 768][tp * 256:(tp + 1) * 256, base % 768:base % 768 + W]
                              .rearrange("(i p) c -> p i c", p=128),
                              src, reads=rd, writes=[('Vl_d', base, tp)])
                    else:
                        T.dma('sp', Vc_d[:, base:base + W].rearrange("(i p) c -> p i c", p=128),
                              src, reads=rd, writes=[('Vc_d', base)])

                for ci, cg in enumerate(CG_ORDER):
                    b = ci % 2
                    if ci + 1 < len(CG_ORDER):
                        proj_load_w(li, wb, ci + 1)
                    segs = CGS[cg]
                    cur = {}
                    n_tt = 8 if (not with_ctx and segs[0][3] in ('Q', 'G')) else 10
                    for tt in range(n_tt):
                        lat = tt < 8
                        tp = tt // 2
                        pji = cnt['pj'] % 3
                        cnt['pj'] += 1
                        mm_group(pj[pji][:], [(nT[:, k, tt * 128:(tt + 1) * 128], wb[b][:, k, :]) for k in range(16)],
                                 reads=[('pwb', b)] + [('nT', tt, k) for k in range(16)], writes=[('pj', pji)])
                        cnt['n'] += 1
                        now = cnt['n']
                        run_deferred(now, LAG)
                        for si, seg in enumerate(segs):
                            (kind, c0, W, dest, base, proc) = seg
                            src = pj[pji][:, c0:c0 + W]
                            if kind == 'R':
                                if tt % 2 == 0:
                                    cur[si] = cnt['R'] % 3
                                    cnt['R'] += 1
                                sR = cur[si]
                                T.op('act', lambda e: e.activation(out=stgR[sR][:, tt % 2, 0:W], in_=src, func=AF.Copy),
                                     reads=[('pj', pji)], writes=[('stgR', sR, tt % 2)])
                                if tt % 2 == 1:
                                    flush_R(seg, sR, tp)
                                continue
                            nh = W // 128
                            if tt % 2 == 0:
                                cur[si] = cnt['T'] % 3
                                cnt['T'] += 1
                            sT = cur[si]
                            xbi = cnt['xb'] % NXB
                            cnt['xb'] += 1
                            xbv = xb[xbi][:, 0:W]
                            need_rope = lat and proc in ('rope_s', 'rope_h', 'norm_q', 'norm_k')
                            need_norm = proc in ('norm_q', 'norm_k')
                            if proc == 'silu':
                                T.op('act', lambda e: e.activation(out=xbv, in_=src, func=AF.Silu),
                                     reads=[('pj', pji)], writes=[('xb', xbi)])
                            elif not need_rope and not need_norm:
                                T.op('act', lambda e: e.activation(out=xbv, in_=src, func=AF.Copy),
                                     reads=[('pj', pji)], writes=[('xb', xbi)])
                            else:
                                xi = cnt['xs'] % 2
                                cnt['xs'] += 1
                                xsv = xs[xi][:, 0:W]
                                tuv = tu[xi][:, 0:W]
                                T.op('act', lambda e: e.activation(out=xsv, in_=src, func=AF.Copy),
                                     reads=[('pj', pji)], writes=[('xs', xi)])
                                if need_norm:
                                    ni = cnt['nst'] % 8
                                    cnt['nst'] += 1
                                    gb = gqb if proc == 'norm_q' else gkb
                                    x3 = xsv.rearrange("p (a b) -> p a b", b=128)
                                    t3 = tuv.rearrange("p (a b) -> p a b", b=128)
                                    T.op('dve', lambda e: e.tensor_tensor(out=tuv, in0=xsv, in1=xsv, op=ALU.mult),
                                         reads=[('xs', xi)], writes=[('tu', xi)])
                                    T.op('dve', lambda e: e.tensor_reduce(out=nst[:, ni, 0:nh], in_=t3, axis=AX.X, op=ALU.add),
                                         reads=[('tu', xi)], writes=[('pnst', ni)])
                                    T.op('act', lambda e: e.activation(out=nst[:, ni, 0:nh], in_=nst[:, ni, 0:nh], func=AF.Sqrt,
                                                                       bias=ceps[:, 0:1], scale=1.0 / 128),
                                         reads=[('pnst', ni), ('ceps',)], writes=[('pnst', ni)])
                                    T.op('dve', lambda e: e.reciprocal(out=nst[:, ni, 0:nh], in_=nst[:, ni, 0:nh]),
                                         reads=[('pnst', ni)], writes=[('pnst', ni)])
                                    T.op('dve', lambda e: e.tensor_tensor(out=x3, in0=x3,
                                                                          in1=nst[:, ni, 0:nh].unsqueeze(2).to_broadcast([128, nh, 128]),
                                                                          op=ALU.mult),
                                         reads=[('xs', xi), ('pnst', ni)], writes=[('xs', xi)])
                                    if need_rope:
                                        T.op('dve', lambda e: e.tensor_tensor(out=x3, in0=x3,
                                                                              in1=gb[:].unsqueeze(1).to_broadcast([128, nh, 128]),
                                                                              op=ALU.mult),
                                             reads=[('xs', xi), ('gqb',), ('gkb',)], writes=[('xs', xi)])
                                    else:
                                        T.op('dve', lambda e: e.tensor_tensor(out=xbv.rearrange("p (a b) -> p a b", b=128), in0=x3,
                                                                              in1=gb[:].unsqueeze(1).to_broadcast([128, nh, 128]),
                                                                              op=ALU.mult),
                                             reads=[('xs', xi), ('gqb',), ('gkb',)], writes=[('xb', xbi)])
                                if need_rope:
                                    if proc == 'rope_s':
                                        hd, hf, co = 64, 32, 256
                                    else:
                                        hd, hf, co = 128, 64, 0
                                    nsub = W // hd
                                    x4 = xsv.rearrange("p (a t f) -> p a t f", t=2, f=hf)
                                    u4 = tuv.rearrange("p (a t f) -> p a t f", t=2, f=hf)
                                    x3r = xsv.rearrange("p (a b) -> p a b", b=hd)
                                    Ct = rope[:, tt, co:co + hd].unsqueeze(1).to_broadcast([128, nsub, hd])
                                    Sn = rope[:, tt, co + hd:co + hd + hf].unsqueeze(1).to_broadcast([128, nsub, hf])
                                    Sp = rope[:, tt, co + hd + hf:co + 2 * hd].unsqueeze(1).to_broadcast([128, nsub, hf])
                                    T.op('dve', lambda e: e.tensor_tensor(out=u4[:, :, 0, :], in0=x4[:, :, 1, :], in1=Sn, op=ALU.mult),
                                         reads=[('xs', xi), ('rope', tt)], writes=[('tu', xi)])
                                    T.op('dve', lambda e: e.tensor_tensor(out=u4[:, :, 1, :], in0=x4[:, :, 0, :], in1=Sp, op=ALU.mult),
                                         reads=[('xs', xi), ('rope', tt)], writes=[('tu', xi)])
                                    T.op('dve', lambda e: e.tensor_tensor(out=x3r, in0=x3r, in1=Ct, op=ALU.mult),
                                         reads=[('xs', xi), ('rope', tt)], writes=[('xs', xi)])
                                    T.op('dve', lambda e: e.tensor_tensor(out=xbv, in0=xsv, in1=tuv, op=ALU.add),
                                         reads=[('xs', xi), ('tu', xi)], writes=[('xb', xbi)])
                            dq.append((now, (lambda it=(xbi, nh, sT, tt): emit_transposes(it))))
                            if tt % 2 == 1:
                                dq.append((now, (lambda seg=seg, sT=sT, tp=tp: flush_T(seg, sT, tp))))
                    if cg == 9:
                        dq.append((cnt['n'], exchange))
                run_deferred(10 ** 9, 0)
            T.retire()

        def attn_phase(li, l_abs, with_ctx):
            lam_init = 0.8 - 0.6 * math.exp(-0.3 * l_abs)
            with contextlib.ExitStack() as ph:
                QTh = [sb("QTh%d" % i, [128, NTOK], BF16, ph) for i in range(2)]
                GTh = [sb("GTh%d" % i, [128, NTOK], BF16, ph) for i in range(2)]
                QZ1 = [sb("QZ%d" % i, [128, NTOK], BF16, ph) for i in range(2)]
                QZ = [[QTh[i], QZ1[i]] for i in range(2)]
                KTh = [sb("KTh%d" % i, [128, 3328], BF16, ph) for i in range(2)]
                Vh = [sb("Vh%d" % i, [128, 26, 129], BF16, ph) for i in range(2)]
                Ef = [sb("Ef%d" % i, [128, 512], BF16, ph) for i in range(3)]
                El = [sb("El%d" % i, [128, 8, 128], BF16, ph) for i in range(2)]
                tmpS = [sb("tmpS%d" % i, [128, 6, 128], F32, ph) for i in range(2)]
                tabC = [sb("tabC%d" % i, [128, 6, 128], F32, ph) for i in range(2)]
                btb = sb("btb", [128, 6, 128], F32, ph)
                stash = sb("stash", [128, 10, 128], F32, ph)
                onb4 = [sb("onb%d" % i, [128, 4, 128], BF16, ph) for i in range(3)]
                a4 = [sb("af%d" % i, [128, 4, 128], F32, ph) for i in range(3)]
                af = a4
                sq4 = tmpS[0]
                pvsb = [sb("pvsb%d" % i, [128, 4, 132], F32, ph) for i in range(3)]
                sm = sb("asm", [128, 16, 16], F32, ph)
                lp = {}
                lamt = a4[2][:, 0:2, :].rearrange("p a b -> p (a b)")
                lsc = sb("lsc", [128, 8], F32, ph)
                subg = sb("subg", [128, 128], F32, ph)
                sinkt = sb("sinkt", [128, 4], F32, ph)
                esink = sb("esink", [128, 4], F32, ph)
                PS = ps("PS", [128, 6, 512], F32, ph)
                ptos = [ps("pto%d" % i, [128, 1024], BF16, ph) for i in range(2)]
                cnt = {'sf': 0, 'ef': 0, 'sl': 0, 'pvl': 0, 'onb': 0, 'sm': 0, 'pto': 0, 'af': 0, 'tab': 0, 'q': 0, 'kv': 0, 'pvsb': 0}

                T.dma('sp', lamt, _bcast_rows(lamv[li:li + 1, :], 128), writes=[('lamt',), ('a4', 2)])
                T.dma('sp', subg[:], _bcast_rows(subln_g[li:li + 1, :], 128), writes=[('subg',)])
                T.dma('sp', sinkt[:], _bcast_rows(sink[li:li + 1, :], 128), writes=[('sinkt',)])
                T.dma('sp', btb[:], btab.rearrange("n k q -> k n q"), writes=[('btb',)])
                for i in range(2):
                    T.op('dve', lambda e: e.memset(Vh[i][:, :, 128:129], 1.0), writes=[('Vh', i)])
                    for p in range(2):
                        T.op('pool', lambda e: e.memset(QZ[i][p][:], 0.0), writes=[('QZ', i), ('QTh', i)])
                    T.op('dve', lambda e: e.scalar_tensor_tensor(out=a4[0][:, 0, 0:64],
                                                                 in0=lamt[:, i * 128:i * 128 + 64], scalar=1.0,
                                                                 in1=lamt[:, i * 128 + 64:i * 128 + 128],
                                                                 op0=ALU.mult, op1=ALU.mult,
                                                                 accum_out=lsc[:, i:i + 1]),
                         reads=[('lamt',)], writes=[('lsc', i), ('a4', 0)])
                T.op('act', lambda e: e.activation(out=lsc[:, 2:4], in_=lsc[:, 0:2], func=AF.Exp),
                     reads=[('lsc', 0), ('lsc', 1)], writes=[('lsc', 2)])
                T.op('act', lambda e: e.activation(out=esink[:], in_=sinkt[:], func=AF.Exp),
                     reads=[('sinkt',)], writes=[('esink',)])
                T.op('dve', lambda e: e.tensor_tensor(out=lsc[:, 4:5], in0=lsc[:, 2:3], in1=lsc[:, 3:4], op=ALU.subtract),
                     reads=[('lsc', 2)], writes=[('lsc', 4)])
                T.op('dve', lambda e: e.tensor_scalar(out=lsc[:, 5:6], in0=lsc[:, 4:5], scalar1=lam_init, scalar2=-1.0,
                                                      op0=ALU.add, op1=ALU.mult),
                     reads=[('lsc', 4)], writes=[('nlam',)])
                T.op('dve', lambda e: e.tensor_scalar(out=subg[:], in0=subg[:], scalar1=(1.0 - lam_init), scalar2=None,
                                                      op0=ALU.mult),
                     reads=[('subg',)], writes=[('subg',)])

                def load_kv(m, g, vbase, local):
                    bi = cnt['kv'] % 2
                    cnt['kv'] += 1
                    kt, vt = KTh[bi], Vh[bi]
                    pi = g // 6
                    gl = g % 6
                    lrows = slice(gl * 128, (gl + 1) * 128)
                    rows = slice(g * 128, (g + 1) * 128)
                    vcols = slice(vbase, vbase + 128)
                    vlc = slice(vbase % 768, vbase % 768 + 128)
                    kl_reads = [('KTl_d', b_, tp) for b_ in (0, 4, 6, 10) for tp in range(4)]
                    vl_reads = [('Vl_d', b_, tp) for b_ in (0, 512, 768, 1280) for tp in range(4)]
                    if not local:
                        KTa3 = KTa_d[pi].rearrange("(r x) t -> x r t", r=2)
                        T.dma('sp', kt[:, 0:2048].rearrange("p (r t) -> p r t", r=2), KTa3[lrows, :, :],
                              reads=[('KTa_d', pi)], writes=[('KTh', bi)])
                        T.dma('sp', vt[:, 0:16, 0:128], Va_d[pi][:, vlc].rearrange("(j p) c -> p j c", p=128),
                              reads=[('Va_d', pi)], writes=[('Vh', bi)])
                    else:
                        T.dma('sp', kt[:, 768:1024], KTa_d[pi][gl * 128:(gl + 1) * 128, 768:1024],
                              reads=[('KTa_d', pi)], writes=[('KTh', bi)])
                        T.dma('sp', kt[:, 1024:1280], KTa_d[pi][768 + gl * 128:768 + (gl + 1) * 128, 0:256],
                              reads=[('KTa_d', pi)], writes=[('KTh', bi)])
                        T.dma('sp', kt[:, 2304:3328], KTl_d[pi][lrows, :],
                              reads=kl_reads, writes=[('KTh', bi)])
                        T.dma('sp', vt[:, 6:10, 0:128], Va_d[pi][768:1280, vlc].rearrange("(j p) c -> p j c", p=128),
                              reads=[('Va_d', pi)], writes=[('Vh', bi)])
                        T.dma('sp', vt[:, 18:26, 0:128], Vl_d[pi][:, vlc].rearrange("(j p) c -> p j c", p=128),
                              reads=vl_reads, writes=[('Vh', bi)])
                    T.dma('sp', kt[:, 2048:2304], KTc_d[rows, :],
                          reads=[('KTc_d', b_) for b_ in (0, 4, 6, 10)], writes=[('KTh', bi)])
                    T.dma('sp', vt[:, 16:18, 0:128], Vc_d[:, vcols].rearrange("(j p) c -> p j c", p=128),
                          reads=[('Vc_d', b_) for b_ in (0, 512, 768, 1280)], writes=[('Vh', bi)])
                    return bi

                def load_q(chunk, split=False):
                    bi = cnt['q'] % 2
                    cnt['q'] += 1
                    rows = slice(chunk * 128, (chunk + 1) * 128)
                    qreads = [('QT_d', b_, tp) for b_ in (0, 4, 8, 12) for tp in range(5)]
                    if split:
                        for p in range(2):
                            T.dma('sp', QZ[bi][p][64 * p:64 * p + 64, :], QT_d[chunk * 128 + 64 * p:chunk * 128 + 64 * p + 64, :],
                                  reads=qreads, writes=[('QZ', bi), ('QTh', bi)])
                    else:
                        T.dma('sp', QTh[bi][:], QT_d[rows, :], reads=qreads, writes=[('QTh', bi), ('QZ', bi)])
                    T.dma('sp', GTh[bi][:], GT_d[rows, :],
                          reads=[('GT_d', b_, tp) for b_ in (0, 4, 8, 12) for tp in range(5)], writes=[('GTh', bi)])
                    return bi

                def new_sm():
                    i = cnt['sm'] % 16
                    cnt['sm'] += 1
                    return i

                def new_pvsb():
                    i = cnt['pvsb'] % 3
                    cnt['pvsb'] += 1
                    return i

                def post_from_sbuf(kind, chunk, qb, q0, nq, pi, p=None, sink_ap=None):
                    pv = pvsb[pi]
                    pk = [('pvsb', pi)]
                    si = new_sm()
                    rd = sm[:, si, 0:nq]
                    if sink_ap is not None:
                        T.op('dve', lambda e: e.tensor_scalar(out=rd, in0=pv[:, 0:nq, 128], scalar1=sink_ap, scalar2=None,
                                                              op0=ALU.add),
                             reads=pk + [('esink',)], writes=[('asm', si)])
                        T.op('dve', lambda e: e.reciprocal(out=rd, in_=rd), reads=[('asm', si)], writes=[('asm', si)])
                    else:
                        T.op('dve', lambda e: e.reciprocal(out=rd, in_=pv[:, 0:nq, 128]), reads=pk, writes=[('asm', si)])
                    rd_b = rd.unsqueeze(2).to_broadcast([128, nq, 128])
                    pvv = pv[:, 0:nq, 0:128]
                    if kind == 'A' and p == 0:
                        T.op('dve', lambda e: e.tensor_tensor(out=stash[:, q0:q0 + nq, :], in0=pvv, in1=rd_b, op=ALU.mult),
                             reads=pk + [('asm', si)], writes=[('stash', q0 + qi) for qi in range(nq)])
                        return
                    oi = cnt['onb'] % 3
                    cnt['onb'] += 1
                    ob = onb4[oi][:, 0:nq, :]
                    part1b = None
                    if kind == 'A':
                        ai = cnt['af'] % 3
                        cnt['af'] += 1
                        a = a4[ai][:, 0:nq, :]
                        r1 = sm[:, si, 4:4 + nq]
                        ss = sm[:, si, 8:8 + nq]
                        T.op('dve', lambda e: e.tensor_scalar(out=r1, in0=rd, scalar1=lsc[:, 5:6], scalar2=None, op0=ALU.mult),
                             reads=[('asm', si), ('nlam',)], writes=[('asm', si)])
                        T.op('dve', lambda e: e.tensor_tensor(out=a, in0=pvv, in1=r1.unsqueeze(2).to_broadcast([128, nq, 128]),
                                                              op=ALU.mult),
                             reads=pk + [('asm', si)], writes=[('a4', ai)])
                        T.op('dve', lambda e: e.tensor_tensor(out=a, in0=a, in1=stash[:, q0:q0 + nq, :], op=ALU.add),
                             reads=[('a4', ai)] + [('stash', q0 + qi) for qi in range(nq)], writes=[('a4', ai)])
                        T.op('dve', lambda e: e.tensor_tensor(out=sq4[:, 0:nq, :], in0=a, in1=a, op=ALU.mult),
                             reads=[('a4', ai)], writes=[('tmpS', 0)])
                        T.op('dve', lambda e: e.tensor_reduce(out=ss, in_=sq4[:, 0:nq, :], axis=AX.X, op=ALU.add),
                             reads=[('tmpS', 0)], writes=[('asm', si)])

                        def part1b():
                            T.op('act', lambda e: e.activation(out=ss, in_=ss, func=AF.Ln, bias=ceps[:, 0:1], scale=1.0 / 128),
                                 reads=[('asm', si), ('ceps',)], writes=[('asm', si)])
                            T.op('act', lambda e: e.activation(out=ss, in_=ss, func=AF.Exp, scale=-0.5),
                                 reads=[('asm', si)], writes=[('asm', si)])
                            T.op('dve', lambda e: e.tensor_tensor(out=a, in0=a,
                                                                  in1=subg[:].unsqueeze(1).to_broadcast([128, nq, 128]),
                                                                  op=ALU.mult),
                                 reads=[('a4', ai), ('subg',)], writes=[('a4', ai)])
                            T.op('dve', lambda e: e.tensor_tensor(out=ob, in0=a,
                                                                  in1=ss.unsqueeze(2).to_broadcast([128, nq, 128]),
                                                                  op=ALU.mult),
                                 reads=[('a4', ai), ('asm', si)], writes=[('onb', oi)])
                    else:
                        T.op('dve', lambda e: e.tensor_tensor(out=ob, in0=pvv, in1=rd_b, op=ALU.mult),
                             reads=pk + [('asm', si)], writes=[('onb', oi)])
                    def part2():
                        if part1b is not None:
                            part1b()
                        ti = cnt['pto'] % 2
                        cnt['pto'] += 1
                        pt_ = ptos[ti][:, 0:nq * 128]

                        def trf(pe):
                            ins = None
                            for qi in range(nq):
                                ins = pe.transpose(ptos[ti][:, qi * 128:(qi + 1) * 128], onb4[oi][:, qi, :], identb[:])
                            return ins
                        T.op('pe', trf, reads=[('onb', oi), ('identb',)], writes=[('pto', ti)])
                        T.op('dve', lambda e: e.tensor_tensor(out=nT[:, chunk, q0 * 128:(q0 + nq) * 128], in0=pt_,
                                                              in1=GTh[qb][:, q0 * 128:(q0 + nq) * 128], op=ALU.mult),
                             reads=[('pto', ti), ('GTh', qb)], writes=[('nT', q0 + qi, chunk) for qi in range(nq)])
                    lp['defer'].append([lp['t'] + (14 if kind == 'A' else 8), part2])

                NSB = 4
                LOOK = 3

                def full_iters(hb, q0, kblocks, ksl, scale, post, batch4, qz=None):
                    nq = 2
                    N = 256
                    npair = len(kblocks) // 2
                    its = []
                    for i in range(npair):
                        it = {'look': LOOK, 'type': 'F'}

                        def S(it=it, i=i):
                            qb, kb = hb['qb'], hb['kb']
                            qsrc = QTh[qb] if qz is None else QZ[qb][qz]
                            qkey = ('QTh', qb) if qz is None else ('QZ', qb)
                            sbk = cnt['sf'] % NSB
                            cnt['sf'] += 1
                            it['sbk'] = sbk

                            def sf(pe):
                                ins = None
                                for hh in range(2):
                                    kcol, vt = kblocks[2 * i + hh]
                                    ins = pe.matmul(PS[:, sbk, hh * 256:hh * 256 + N], KTh[kb][ksl, kcol:kcol + 128],
                                                    qsrc[ksl, q0 * 128:q0 * 128 + N], start=True, stop=True)
                                return ins
                            T.op('pe', sf, reads=[('KTh', kb), qkey], writes=[('PS', sbk)])

                        def EP(it=it, i=i):
                            qb, kb = hb['qb'], hb['kb']
                            sbk = it['sbk']
                            ei = cnt['ef'] % 3
                            cnt['ef'] += 1
                            T.op('act', lambda e: e.activation(out=Ef[ei][:, 0:512], in_=PS[:, sbk, 0:512], func=AF.Exp, scale=scale),
                                 reads=[('PS', sbk)], writes=[('Ef', ei)])

                            def pvf(pe):
                                ins = None
                                for hh in range(2):
                                    vt = kblocks[2 * i + hh][1]
                                    for qi in range(nq):
                                        ins = pe.matmul(PS[:, 4 + qi, 0:129],
                                                        Ef[ei][:, hh * 256 + qi * 128:hh * 256 + (qi + 1) * 128],
                                                        Vh[kb][:, vt, :], start=(i == 0 and hh == 0),
                                                        stop=(i == npair - 1 and hh == 1))
                                return ins
                            T.op('pe', pvf, reads=[('Ef', ei), ('Vh', kb)], writes=[('PS', 4), ('PS', 5)])
                            if i != npair - 1:
                                return
                            if batch4:
                                slot = q0 % 4
                                if slot == 0:
                                    lp['fpi'] = new_pvsb()
                                pi = lp['fpi']
                            else:
                                slot = 0
                                pi = new_pvsb()
                            if batch4:
                                T.op('dve', lambda e: e.tensor_copy(out=pvsb[pi][:, slot:slot + 2, 0:129], in_=PS[:, 4:6, 0:129]),
                                     reads=[('PS', 4), ('PS', 5)], writes=[('pvsb', pi)])
                            else:
                                T.op('act', lambda e: e.activation(out=pvsb[pi][:, slot:slot + 2, 0:129], in_=PS[:, 4:6, 0:129],
                                                                   func=AF.Copy),
                                     reads=[('PS', 4), ('PS', 5)], writes=[('pvsb', pi)])
                            if batch4:
                                if slot == 2:
                                    post(hb['qb'], q0 - 2, 4, pi)
                            else:
                                post(hb['qb'], q0, 2, pi)
                        it['S'] = S
                        it['EP'] = EP
                        its.append(it)
                    return its

                def local_iters(hb, specs, scale, post):
                    its = []
                    for j in range(8):
                        it = {'look': 1, 'type': 'L'}

                        def S(it=it, j=j):
                            qb, kb = hb['qb'], hb['kb']
                            tabled, tab_ap, tab_reads, plain = specs[j]()
                            blocks = tabled + plain
                            nbt = len(tabled)
                            nb = len(blocks)
                            s_ = cnt['sl'] % 2
                            cnt['sl'] += 1
                            it['s'] = s_
                            it['blocks'] = blocks
                            Sv = PS[:, 2 * s_:2 * s_ + 2, :].rearrange("p a (b c) -> p (a b) c", c=128)

                            def sf(pe):
                                ins = None
                                for i, (kcol, vt) in enumerate(blocks):
                                    ins = pe.matmul(Sv[:, i, :], KTh[kb][:, kcol:kcol + 128], QTh[qb][:, j * 128:(j + 1) * 128],
                                                    start=True, stop=True)
                                return ins
                            T.op('pe', sf, reads=[('KTh', kb), ('QTh', qb)], writes=[('PS', 2 * s_), ('PS', 2 * s_ + 1)])
                            el = El[s_]
                            T.op('dve', lambda e: e.scalar_tensor_tensor(out=tmpS[s_][:, 0:nbt, :], in0=Sv[:, 0:nbt, :], scalar=scale,
                                                                         in1=tab_ap, op0=ALU.mult, op1=ALU.add),
                                 reads=[('PS', 2 * s_), ('PS', 2 * s_ + 1)] + tab_reads, writes=[('tmpS', s_)])
                            T.op('act', lambda e: e.activation(out=el[:, 0:nbt, :], in_=tmpS[s_][:, 0:nbt, :], func=AF.Exp),
                                 reads=[('tmpS', s_)], writes=[('El', s_, 0)])
                            T.op('act', lambda e: e.activation(out=el[:, nbt:nb, :], in_=Sv[:, nbt:nb, :], func=AF.Exp, scale=scale),
                                 reads=[('PS', 2 * s_), ('PS', 2 * s_ + 1)], writes=[('El', s_, 1)])

                        def EP(it=it, j=j):
                            kb = hb['kb']
                            s_, blocks = it['s'], it['blocks']
                            nb = len(blocks)
                            el = El[s_]
                            pvi = 4 + (cnt['pvl'] % 2)
                            cnt['pvl'] += 1

                            def pvf(pe):
                                ins = None
                                for i, (kcol, vt) in enumerate(blocks):
                                    ins = pe.matmul(PS[:, pvi, 0:129], el[:, i, :], Vh[kb][:, vt, :], start=(i == 0),
                                                    stop=(i == nb - 1))
                                return ins
                            T.op('pe', pvf, reads=[('El', s_, 0), ('El', s_, 1), ('Vh', kb)], writes=[('PS', pvi)])
                            if j % 4 == 0:
                                lp['pi'] = new_pvsb()
                            pi = lp['pi']
                            T.op('act', lambda e: e.activation(out=pvsb[pi][:, j % 4, 0:129], in_=PS[:, pvi, 0:129], func=AF.Copy),
                                 reads=[('PS', pvi)], writes=[('pvsb', pi)])
                            if j % 4 == 3:
                                post(hb['qb'], j - 3, 4, pi)
                        it['S'] = S
                        it['EP'] = EP
                        its.append(it)
                    return its

                def run_stream(its):
                    n = len(its)
                    nxt = 0
                    lp['defer'] = []
                    for t in range(n):
                        lp['t'] = t
                        while lp['defer'] and lp['defer'][0][0] <= t:
                            lp['defer'].pop(0)[1]()
                        while nxt < n:
                            look = its[nxt]['look']
                            if nxt > 0 and its[nxt]['type'] != its[nxt - 1]['type']:
                                look = 0
                            if nxt > t + look:
                                break
                            its[nxt]['S']()
                            nxt += 1
                        if 'PRE' in its[t]:
                            while lp['defer']:
                                lp['defer'].pop(0)[1]()
                            its[t]['PRE']()
                        its[t]['EP']()
                    while lp['defer']:
                        lp['defer'].pop(0)[1]()

                ALLK = [(a * 128, a) for a in range(16)] + [(2048 + c * 128, 16 + c) for c in range(2)]
                CTXK = [(2048 + c * 128, 16 + c) for c in range(2)]
                A128 = slice(0, 128)

                def own(jj):
                    return (2304 + jj * 128, 18 + jj)

                def allk(a):
                    return (a * 128, a)

                sc = 128 ** -0.5
                stream = []
                heads = []

                for hq in range(4):
                    hb = {}
                    its = []
                    for p in range(2):
                        pst = (lambda qb, q0, nq, pi, p=p, hq=hq: post_from_sbuf('A', hq, qb, q0, nq, pi, p=p))
                        for g in range(4):
                            its += full_iters(hb, 2 * g, ALLK, A128, 0.125, pst, True, qz=p)
                        if with_ctx:
                            its += full_iters(hb, 8, CTXK, A128, 0.125, pst, False, qz=p)
                    heads.append((hb, (lambda hq=hq: (load_kv(0, hq, hq * 128, False), load_q(hq, split=True))), its))
                for kvh in range(2):
                    for hq in (2 * kvh, 2 * kvh + 1):
                        hb = {}
                        pst = (lambda qb, q0, nq, pi, hq=hq: post_from_sbuf('G', 4 + hq, qb, q0, nq, pi,
                                                                             sink_ap=esink[:, hq:hq + 1]))

                        def bspec(j):
                            if j == 0:
                                return ([allk(7), own(1)], btb[:, 2:4, :], [('btb',)], [own(j)] + CTXK)
                            if j == 7:
                                return ([own(6), allk(8)], btb[:, 4:6, :], [('btb',)], [own(j)] + CTXK)
                            return ([own(j - 1), own(j + 1)], btb[:, 0:2, :], [('btb',)], [own(j)] + CTXK)
                        its = local_iters(hb, [(lambda j=j: bspec(j)) for j in range(8)], sc, pst)
                        if with_ctx:
                            its += full_iters(hb, 8, CTXK, A128, sc, pst, False)
                        if hq == 2 * kvh:
                            ld = (lambda kvh=kvh, hq=hq: (load_kv(1, 4 + kvh, 512 + kvh * 128, True), load_q(4 + hq)))
                        else:
                            ld = (lambda hq=hq: (None, load_q(4 + hq)))
                        heads.append((hb, ld, its))
                for hq in range(4):
                    hb = {}
                    pst = (lambda qb, q0, nq, pi, hq=hq: post_from_sbuf('G', 8 + hq, qb, q0, nq, pi))

                    def cspec(j, hq=hq):
                        dl = C_DELTAS[j]
                        tabled = []
                        for dlt in dl:
                            jj = j + dlt
                            if jj < 0:
                                tabled.append(allk(8 + jj))
                            elif jj > 7:
                                tabled.append(allk(jj))
                            else:
                                tabled.append(own(jj))
                        ti = cnt['tab'] % 2
                        cnt['tab'] += 1
                        n = len(dl)
                        T.dma('sp', tabC[ti][:, 0:n, :], ctab[li, hq, C_TAB0[j]:C_TAB0[j] + n].rearrange("n k q -> k n q"),
                              writes=[('tabC', ti)])
                        return (tabled, tabC[ti][:, 0:n, :], [('tabC', ti)], CTXK)
                    its = local_iters(hb, [(lambda j=j, cspec=cspec: cspec(j)) for j in range(8)], sc, pst)
                    if with_ctx:
                        its += full_iters(hb, 8, CTXK, A128, sc, pst, False)
                    heads.append((hb, (lambda hq=hq: (load_kv(2, 6 + hq, 768 + hq * 128, True), load_q(8 + hq))), its))
                for kvh in range(2):
                    for hq in (2 * kvh, 2 * kvh + 1):
                        hb = {}
                        pst = (lambda qb, q0, nq, pi, hq=hq: post_from_sbuf('G', 12 + hq, qb, q0, nq, pi))
                        its = []
                        for g in range(4):
                            its += full_iters(hb, 2 * g, ALLK, A128, sc, pst, True)
                        if with_ctx:
                            its += full_iters(hb, 8, CTXK, A128, sc, pst, False)
                        if hq == 2 * kvh:
                            ld = (lambda kvh=kvh, hq=hq: (load_kv(3, 10 + kvh, 1280 + kvh * 128, False), load_q(12 + hq)))
                        else:
                            ld = (lambda hq=hq: (None, load_q(12 + hq)))
                        heads.append((hb, ld, its))

                state = {'kb': None}

                def do_load(hi):
                    hb, ld, _ = heads[hi]
                    kbn, qbn = ld()
                    if kbn is not None:
                        state['kb'] = kbn
                    hb['kb'] = state['kb']
                    hb['qb'] = qbn
                for hi, (hb, ld, its) in enumerate(heads):
                    def pre(hi=hi):
                        if hi + 1 < len(heads):
                            do_load(hi + 1)
                    its[16 if len(its) >= 32 else 0]['PRE'] = pre
                    stream += its
                do_load(0)
                run_stream(stream)
            T.retire()

        def out_phase(li, with_ctx):
            ntile = 10 if with_ctx else 8
            with contextlib.ExitStack() as ph:
                wo = sb("wo", [128, 16, D], BF16, ph)
                gp = sb("gp", [128, D], F32, ph)
                pgb = sb("pgb", [128, D], F32, ph)
                st2 = sb("ost", [128, 10, 8], F32, ph)
                py = [ps("py%d" % i, [128, 4, 512], F32, ph) for i in range(2)]
                for cg in range(4):
                    T.dma('pool', wo[:, :, cg * 512:(cg + 1) * 512],
                          w_out[li, :, cg * 512:(cg + 1) * 512].rearrange("(k p) c -> p k c", p=128),
                          writes=[('wo', cg)])
                T.dma('sp', pgb[:], _bcast_rows(post_g[li:li + 1, :], 128), writes=[('pgb',)])

                def make_gp(r):
                    T.dma('sp', gp[:], _bcast_rows(gate_d[r:r + 1, :], 128), reads=[('gate_d', g_) for g_ in range(4)], writes=[('gp',)])
                    T.op('dve', lambda e: e.tensor_tensor(out=gp[:], in0=gp[:], in1=pgb[:], op=ALU.mult),
                         reads=[('gp',), ('pgb',)], writes=[('gp',)])

                make_gp(0)
                for tt in range(ntile):
                    if tt == 8:
                        make_gp(1)
                    pi = tt % 2
                    pyv = py[pi]
                    for cg in range(4):
                        mm_group(pyv[:, cg, :], [(nT[:, k, tt * 128:(tt + 1) * 128], wo[:, k, cg * 512:(cg + 1) * 512])
                                                 for k in range(16)],
                                 reads=[('wo', cg)] + [('nT', tt, k) for k in range(16)], writes=[('py', pi, cg)])
                    for cg in range(4):
                        T.op('act', lambda e: e.activation(out=pgb_junk[:], in_=pyv[:, cg, :],
                                                           func=AF.Square, accum_out=st2[:, tt, cg:cg + 1]),
                             reads=[('py', pi, cg)], writes=[('ost', tt, cg)])
                    T.op('dve', lambda e: e.tensor_reduce(out=st2[:, tt, 4:5], in_=st2[:, tt, 0:4], axis=AX.X, op=ALU.add),
                         reads=[('ost', tt, c) for c in range(4)], writes=[('ost', tt, 4)])
                    T.op('act', lambda e: e.activation(out=st2[:, tt, 5:6], in_=st2[:, tt, 4:5], func=AF.Sqrt,
                                                       bias=ceps[:, 0:1], scale=1.0 / D),
                         reads=[('ost', tt, 4), ('ceps',)], writes=[('ost', tt, 5)])
                    T.op('dve', lambda e: e.reciprocal(out=st2[:, tt, 6:7], in_=st2[:, tt, 5:6]),
                         reads=[('ost', tt, 5)], writes=[('ost', tt, 6)])
                    for cg in range(4):
                        T.op('dve', lambda e: e.tensor_tensor(out=pyv[:, cg, :], in0=pyv[:, cg, :],
                                                              in1=gp[:, cg * 512:(cg + 1) * 512], op=ALU.mult),
                             reads=[('py', pi, cg), ('gp',), ('ost', tt, cg)], writes=[('py', pi, cg)])
                    hv = hview(tt)
                    T.op('dve', lambda e: e.scalar_tensor_tensor(out=hv, in0=pyv[:].rearrange("p a b -> p (a b)"),
                                                                 scalar=st2[:, tt, 6:7], in1=hv, op0=ALU.mult, op1=ALU.add),
                         reads=[('py', pi, c) for c in range(4)] + [('ost', tt, 6), ('h', tt)], writes=[('h', tt)])
            T.retire()

        pgb_junk = sb("sqjunk", [128, 512], BF16)

        import os as _os
        _stop = _os.environ.get("K_STOP", "")
        for li, l_abs in enumerate(layer_ids):
            with_ctx = l_abs < 3
            if _stop == "setup":
                break
            if _stop == "mod":
                break
            with contextlib.ExitStack() as wsc:
                wbuf = [sb("pwb%d" % i, [128, 16, 512], BF16, wsc) for i in range(2)]
                T.persist.add('pwb')
                proj_load_w(li, wbuf, 0)
                mod_norm_phase(li)
                T.persist.discard('pwb')
                if _stop == "norm":
                    break
                proj_phase(li, wbuf, with_ctx)
            if _stop == "proj":
                break
            if _stop == "exch":
                break
            attn_phase(li, l_abs, with_ctx)
            if _stop == "attn":
                break
            out_phase(li, with_ctx)

        for j in range(8):
            T.dma('sp', out_sh[j * 128:(j + 1) * 128, :], h[:, j, :], reads=[('h', j)], writes=[('out', j)])
        if write_ctx:
            for j in range(2):
                T.dma('sp', ctx_out[j * 128:(j + 1) * 128, :], hc[:, j, :], reads=[('h', 8 + j)], writes=[('out', 8 + j)])
        T.finish('sp')
        build.trk = T
        build.stats = dict(cnt={k: v for k, v in T.cnt.items() if not isinstance(k, tuple)}, nwait=T.nwait)
    return nc


def _rope_tables():
    def tab(dim):
        t = np.arange(2048)
        row = (t // GRID_W).astype(np.float32)
        col = (t % GRID_W).astype(np.float32)
        nf = dim // 4
        inv = (np.float32(10000.0) ** (-np.arange(nf, dtype=np.float32) / nf)).astype(np.float32)
        ang = np.concatenate([row[:, None] * inv, col[:, None] * inv], axis=-1).astype(np.float32)
        return np.cos(ang).astype(np.float32), np.sin(ang).astype(np.float32)
    ch, sh_ = tab(128)
    cs, ss = tab(64)
    return np.concatenate([ch, ch, -sh_, sh_, cs, cs, -ss, ss], axis=1).astype(np.float32)


def _c_index(half):
    kp = np.arange(128)[:, None]
    qp = np.arange(128)[None, :]
    rr = np.zeros((NCT, 128, 128), np.int64)
    cc = np.zeros((NCT, 128, 128), np.int64)
    ok = np.zeros((NCT, 128, 128), bool)
    done = set()
    for j in range(8):
        for i, dlt in enumerate(C_DELTAS[j]):
            tid = C_TAB0[j] + i
            if tid in done:
                continue
            done.add(tid)
            jj = j + dlt
            if (jj < 0 and half == 0) or (jj > 7 and half == 1):
                continue
            tq = 8 * half + j
            tk = tq + dlt
            if tk < 0 or tk > 15:
                continue
            kr = 2 * tk + kp // 64
            kc = kp % 64
            r = 2 * tq + qp // 64
            c = qp % 64
            r0 = np.clip(r - 4, 0, 24)
            c0 = np.clip(c - 8, 0, 48)
            valid = (kr >= r0) & (kr < r0 + 8) & (kc >= c0) & (kc < c0 + 16)
            rr[tid] = np.clip(kr - r + 7, 0, 14)
            cc[tid] = np.clip(kc - c + 15, 0, 30)
            ok[tid] = valid
    return rr, cc, ok


def _b_tables(half):
    kp = np.arange(128)[:, None]
    qp = np.arange(128)[None, :]
    negs = np.full((128, 128), NEG, np.float32)
    tri_prev = np.where(qp <= kp, np.float32(0), np.float32(NEG)).astype(np.float32)
    tri_next = np.where(kp <= qp, np.float32(0), np.float32(NEG)).astype(np.float32)
    x_prev = tri_prev if half == 1 else negs
    x_next = tri_next if half == 0 else negs
    return np.stack([tri_prev, tri_next, x_prev, tri_next, tri_prev, x_next]).astype(np.float32)


_PROG_CACHE = {}


def _get_prog(layer_ids, write_ctx):
    key = (tuple(layer_ids), write_ctx)
    if key not in _PROG_CACHE:
        _PROG_CACHE[key] = build(list(layer_ids), write_ctx)
    return _PROG_CACHE[key]


def _in_maps(x_parts, ctx_parts, c, c_ctx, lsl, w_mod, b_mod, pre_norm_g, w_in, lam4, diff_subln_g, win_sink,
             na_rpb, qk_q_norm_g, qk_k_norm_g, w_out, post_norm_g):
    rope = _rope_tables()
    ident_b = np.eye(128, dtype=np.float32).astype(ml_dtypes.bfloat16)
    ident_f = np.eye(128, dtype=np.float32)
    maps = []
    ctab_half = []
    for half in range(2):
        rr, cc, ok = _c_index(half)
        rp = na_rpb[lsl]
        g = rp[:, :, rr, cc]
        ctab_half.append(np.where(ok[None, None], g, np.float32(NEG)).astype(np.float32))
    btabs = [_b_tables(0), _b_tables(1)]
    for core in range(8):
        b, half = core // 2, core % 2
        maps.append({
            "x_sh": np.ascontiguousarray(x_parts[core]),
            "ctx_b": np.ascontiguousarray(ctx_parts[b]),
            "cvec": np.ascontiguousarray(np.stack([c[b], c_ctx]).astype(np.float32)),
            "w_mod": w_mod[lsl], "b_mod": b_mod[lsl], "pre_g": pre_norm_g[lsl], "post_g": post_norm_g[lsl],
            "w_in": w_in[lsl], "w_out": w_out[lsl],
            "lamv": np.ascontiguousarray(lam4[lsl].reshape(-1, 256)),
            "subln_g": diff_subln_g[lsl], "sink": win_sink[lsl], "gq": qk_q_norm_g[lsl], "gk": qk_k_norm_g[lsl],
            "rope_t": np.ascontiguousarray(rope[half * 1024:(half + 1) * 1024]),
            "ident_b": ident_b, "ident_f": ident_f,
            "ctab": ctab_half[half], "btab": btabs[half],
            "sel_in": np.concatenate([np.repeat(np.eye(2, dtype=np.float32)[:, 0:1], 128, 1),
                                      np.repeat(np.eye(2, dtype=np.float32)[:, 1:2], 128, 1)], axis=1),
        })
    return maps


FUSED = True


def kernel(x, c, ctx, c_ctx, w_mod, b_mod, pre_norm_g, w_in, diff_lambda_q1, diff_lambda_k1,
           diff_lambda_q2, diff_lambda_k2, diff_subln_g, win_sink, na_rpb, qk_q_norm_g, qk_k_norm_g,
           w_out, post_norm_g):
    f = lambda a: np.ascontiguousarray(np.asarray(a, dtype=np.float32))
    x, c, ctx, c_ctx = f(x), f(c), f(ctx), f(c_ctx)
    w_mod, b_mod, pre_norm_g, w_in, w_out, post_norm_g = f(w_mod), f(b_mod), f(pre_norm_g), f(w_in), f(w_out), f(post_norm_g)
    diff_subln_g, win_sink, na_rpb, qk_q_norm_g, qk_k_norm_g = f(diff_subln_g), f(win_sink), f(na_rpb), f(qk_q_norm_g), f(qk_k_norm_g)
    lam4 = np.stack([f(diff_lambda_q1), f(diff_lambda_k1), f(diff_lambda_q2), f(diff_lambda_k2)], axis=1)
    x_parts = [x[core // 2, (core % 2) * 1024:(core % 2 + 1) * 1024] for core in range(8)]
    ctx_parts = [ctx[b] for b in range(4)]
    groups = [[0, 1, 2, 3]] if FUSED else [[0], [1], [2], [3]]
    for gi, lids in enumerate(groups):
        last = gi == len(groups) - 1
        lsl = slice(lids[0], lids[-1] + 1)
        nc = _get_prog(lids, not last)
        maps = _in_maps(x_parts, ctx_parts, c, c_ctx, lsl, w_mod, b_mod, pre_norm_g, w_in, lam4, diff_subln_g,
                        win_sink, na_rpb, qk_q_norm_g, qk_k_norm_g, w_out, post_norm_g)
        res = run_bass_kernel_spmd(nc, maps, core_ids=list(range(8)))
        x_parts = [np.asarray(res.results[core]["out_sh"]) for core in range(8)]
        if not last:
            ctx_parts = [np.asarray(res.results[2 * b]["ctx_out"]) for b in range(4)]
    out = np.empty((4, 2048, 2048), np.float32)
    for core in range(8):
        out[core // 2, (core % 2) * 1024:(core % 2 + 1) * 1024] = x_parts[core]
    return out
```

```python
import math
import contextlib
import numpy as np
import ml_dtypes
import concourse.bass as bass
import concourse.mybir as mybir
from concourse.bass_utils import run_bass_kernel_spmd

F32 = mybir.dt.float32
BF16 = mybir.dt.bfloat16
AF = mybir.ActivationFunctionType
ALU = mybir.AluOpType
AX = mybir.AxisListType

NEG = -30000.0
EPS = 1e-6
D = 2048
NLAT = 1024
NCTX = 256
NTOK = NLAT + NCTX
GRID_W = 64
NCT = 27

C_DELTAS = {0: [-2, -1, 0, 1, 2, 3], 1: [-2, -1, 0, 1, 2], 6: [-2, -1, 0, 1, 2], 7: [-3, -2, -1, 0, 1, 2]}
for _j in range(2, 6):
    C_DELTAS[_j] = [-2, -1, 0, 1, 2]
C_TAB0 = {0: 0, 1: 6, 2: 11, 3: 11, 4: 11, 5: 11, 6: 16, 7: 21}


class Trk:
    def __init__(self, nc, st, ndma=40):
        self.nc = nc
        self.E = {'pe': nc.tensor, 'act': nc.scalar, 'dve': nc.vector, 'pool': nc.gpsimd, 'sp': nc.sync}
        self.sems = {}
        for k in ['pe', 'act', 'dve', 'pool']:
            self.sems[k] = st.enter_context(nc.semaphore("s_" + k))
        for i in range(ndma):
            self.sems[('d', i)] = st.enter_context(nc.semaphore("s_d%d" % i))
        self.sems['cc'] = st.enter_context(nc.semaphore("s_cc"))
        self.cnt = {k: 0 for k in self.sems}
        self.known = {e: {} for e in self.E}
        self.ndma = ndma
        self.dnext = 0
        self.res = {}
        self.retired = {}
        self.persist = set()
        self.nwait = 0
        self.log = {e: [] for e in self.E}

    def _st(self, key):
        s = self.res.get(key)
        if s is None:
            s = [None, dict(self.retired)]
            self.res[key] = s
        return s

    def _deps(self, reads, writes):
        need = {}

        def add(tok):
            if tok is None:
                return
            k, v = tok
            if v > need.get(k, 0):
                need[k] = v
        for r in reads:
            add(self._st(r)[0])
        for w in writes:
            s = self._st(w)
            add(s[0])
            for k, v in s[1].items():
                add((k, v))
        return need

    def _wait(self, e, need):
        kn = self.known[e]
        for k, v in need.items():
            if kn.get(k, 0) >= v:
                continue
            self.E[e].wait_ge(self.sems[k], v)
            self.log[e].append(('wait', k, v))
            kn[k] = v
            self.nwait += 1

    def _commit(self, tok, reads, writes):
        k, v = tok
        for r in reads:
            s = self._st(r)
            if v > s[1].get(k, 0):
                s[1][k] = v
        for w in writes:
            self.res[w] = [tok, {}]

    def op(self, e, fn, reads=(), writes=()):
        need = self._deps(reads, writes)
        if e == 'pe':
            need.pop('pe', None)
        self._wait(e, need)
        ins = fn(self.E[e])
        self.cnt[e] += 1
        ins.then_inc(self.sems[e], 1)
        self.log[e].append(('inc', e, 1))
        tok = (e, self.cnt[e])
        self._commit(tok, reads, writes)
        return tok

    def dma(self, q, out, in_, reads=(), writes=(), **kw):
        need = self._deps(reads, writes)
        i = self.dnext
        self.dnext = (i + 1) % self.ndma
        key = ('d', i)
        if self.cnt[key] > 0 and 16 * self.cnt[key] > need.get(key, 0):
            need[key] = 16 * self.cnt[key]
        self._wait(q, need)
        self.E[q].dma_start(out=out, in_=in_, **kw).then_inc(self.sems[key], 16)
        self.log[q].append(('inc', key, 16))
        self.cnt[key] += 1
        tok = (key, 16 * self.cnt[key])
        self._commit(tok, reads, writes)
        return tok

    def allgather(self, in_ap, out_ap, reads, writes):
        self._wait('pool', self._deps(reads, writes))
        self.nc.gpsimd.collective_compute(
            "AllGather", ALU.bypass, replica_groups=[[0, 1], [2, 3], [4, 5], [6, 7]],
            ins=[in_ap.opt()], outs=[out_ap.opt()]).then_inc(self.sems['cc'])
        self.log['pool'].append(('inc', 'cc', 1))
        self.cnt['cc'] += 1
        tok = ('cc', self.cnt['cc'])
        self._commit(tok, reads, writes)
        return tok

    def retire(self):
        for key in list(self.res.keys()):
            if key[0] in self.persist:
                continue
            s = self.res.pop(key)
            toks = dict(s[1])
            if s[0] is not None:
                k, v = s[0]
                if v > toks.get(k, 0):
                    toks[k] = v
            for k, v in toks.items():
                if v > self.retired.get(k, 0):
                    self.retired[k] = v

    def finish(self, e='sp'):
        need = {}
        for k, c in self.cnt.items():
            if c > 0:
                need[k] = c * (16 if isinstance(k, tuple) else 1)
        self._wait(e, need)


def _bcast_rows(ap2d, nparts):
    return bass.AP(tensor=ap2d.tensor, offset=ap2d.offset, ap=[[0, nparts], [1, ap2d.shape[-1]]])


def build(layer_ids, write_ctx):
    NL = len(layer_ids)
    nc = bass.Bass("TRN2", target_bir_lowering=False)

    def din(name, shape, dt=F32):
        return nc.dram_tensor(name, shape, dt, kind="ExternalInput").ap()

    x_sh = din("x_sh", [NLAT, D])
    ctx_b = din("ctx_b", [NCTX, D])
    cvec = din("cvec", [2, D])
    w_mod = din("w_mod", [NL, D, 3 * D])
    b_mod = din("b_mod", [NL, 3 * D])
    pre_g = din("pre_g", [NL, D])
    post_g = din("post_g", [NL, D])
    w_in = din("w_in", [NL, D, 7168])
    w_out = din("w_out", [NL, D, D])
    lamv = din("lamv", [NL, 256])
    subln_g = din("subln_g", [NL, 128])
    sink = din("sink", [NL, 4])
    gq = din("gq", [NL, 128])
    gk = din("gk", [NL, 128])
    rope_t = din("rope_t", [NLAT, 384])
    ident_b = din("ident_b", [128, 128], BF16)
    ident_f = din("ident_f", [128, 128])
    ctab = din("ctab", [NL, 4, NCT, 128, 128])
    btab = din("btab", [6, 128, 128])
    sel_in = din("sel_in", [2, 256])
    out_sh = nc.dram_tensor("out_sh", [NLAT, D], F32, kind="ExternalOutput").ap()
    if write_ctx:
        ctx_out = nc.dram_tensor("ctx_out", [NCTX, D], F32, kind="ExternalOutput").ap()

    QT_d = nc.dram_tensor("QT_d", [16 * 128, NTOK], BF16).ap()
    GT_d = nc.dram_tensor("GT_d", [16 * 128, NTOK], BF16).ap()
    KTl_d = [nc.dram_tensor("KTl_d%d" % i, [6 * 128, NLAT], BF16).ap() for i in range(2)]
    KTc_d = nc.dram_tensor("KTc_d", [12 * 128, NCTX], BF16).ap()
    KTa_d = [nc.dram_tensor("KTa_d%d" % i, [2 * 6 * 128, NLAT], BF16).ap() for i in range(2)]
    Vl_d = [nc.dram_tensor("Vl_d%d" % i, [NLAT, 768], BF16).ap() for i in range(2)]
    Vc_d = nc.dram_tensor("Vc_d", [NCTX, 1536], BF16).ap()
    Va_d = [nc.dram_tensor("Va_d%d" % i, [2 * NLAT, 768], BF16).ap() for i in range(2)]
    gate_d = nc.dram_tensor("gate_d", [2, D], F32).ap()

    with contextlib.ExitStack() as st:
        T = Trk(nc, st)
        T.persist = {'h', 'nT', 'identb', 'identf', 'scT', 'gsT', 'shT', 'ceps',
                     'QT_d', 'GT_d', 'KTl_d', 'KTc_d', 'KTa_d', 'Vl_d', 'Vc_d', 'Va_d', 'gate_d', 'out'}

        uniq = [0]

        def sb(name, shape, dt, stack=st):
            uniq[0] += 1
            return stack.enter_context(nc.sbuf_tensor("%s_%d" % (name, uniq[0]), shape, dt))

        def ps(name, shape, dt, stack):
            uniq[0] += 1
            return stack.enter_context(nc.psum_tensor("%s_%d" % (name, uniq[0]), shape, dt))

        h = sb("h", [128, 8, D], F32)
        hc = sb("hc", [128, 2, D], F32)
        nT = sb("nT", [128, 16, NTOK], BF16)
        identb = sb("identb", [128, 128], BF16)
        identf = sb("identf", [128, 128], F32)
        scT = sb("scT", [128, 16, 2], BF16)
        gsT = sb("gsT", [128, 16, 2], F32)
        shT = sb("shT", [128, 16, 2], F32)
        ceps = sb("ceps", [128, 1], F32)

        def hview(tt):
            return h[:, tt, :] if tt < 8 else hc[:, tt - 8, :]

        for j in range(8):
            T.dma('sp', h[:, j, :], x_sh[j * 128:(j + 1) * 128, :], writes=[('h', j)])
        for j in range(2):
            T.dma('sp', hc[:, j, :], ctx_b[j * 128:(j + 1) * 128, :], writes=[('h', 8 + j)])
        T.dma('sp', identb[:], ident_b, writes=[('identb',)])
        T.dma('sp', identf[:], ident_f, writes=[('identf',)])
        T.op('dve', lambda e: e.memset(ceps[:], EPS), writes=[('ceps',)])
        with contextlib.ExitStack() as ph:
            cT = sb("cT", [128, 16, 2], F32, ph)
            for r in range(2):
                T.dma('sp', cT[:, :, r], cvec[r:r + 1, :].rearrange("o (k p) -> p (o k)", p=128),
                      writes=[('cT', r)], allow_slow_non_contiguous=True)
            T.op('act', lambda e: e.activation(out=scT[:], in_=cT[:], func=AF.Silu),
                 reads=[('cT', 0), ('cT', 1)], writes=[('scT',)])
        T.retire()

        def mm_group(out_ps, pairs, reads, writes):
            def fn(pe):
                n = len(pairs)
                ins = None
                for i, (l, r) in enumerate(pairs):
                    ins = pe.matmul(out_ps, l, r, start=(i == 0), stop=(i == n - 1))
                return ins
            return T.op('pe', fn, reads, writes)

        def mod_phase(li):
            with contextlib.ExitStack() as ph:
                wb = [sb("mwb%d" % i, [128, 16, 512], BF16, ph) for i in range(2)]
                bm = [sb("bm%d" % i, [2, 512], F32, ph) for i in range(2)]
                modrow = sb("modrow", [2, 3 * D], F32, ph)
                pregT = sb("pregT", [128, 16], F32, ph)
                tmpT = sb("tmpT", [128, 16, 2], F32, ph)
                pm = [ps("pm%d" % i, [128, 512], F32, ph) for i in range(2)]
                ptm = ps("ptm", [128, 512], F32, ph)
                T.dma('sp', pregT[:], pre_g[li:li + 1, :].rearrange("o (k p) -> p (o k)", p=128),
                      writes=[('pregT',)], allow_slow_non_contiguous=True)
                for cg in range(8):
                    b = cg % 2
                    cs = slice(cg * 512, (cg + 1) * 512)
                    T.dma('pool', wb[b][:], w_mod[li, :, cs].rearrange("(k p) c -> p k c", p=128),
                          writes=[('mwb', b)])
                    T.dma('sp', bm[b][:], _bcast_rows(b_mod[li:li + 1, cs], 2), writes=[('bm', b)])
                    mm_group(pm[b][0:2, :], [(scT[:, k, :], wb[b][:, k, :]) for k in range(16)],
                             reads=[('mwb', b), ('scT',)], writes=[('pm', b)])
                    T.op('dve', lambda e: e.tensor_tensor(out=modrow[0:2, cs], in0=pm[b][0:2, :],
                                                          in1=bm[b][0:2, :], op=ALU.add),
                         reads=[('pm', b), ('bm', b)], writes=[('modrow', cg)])

                def tr(pe):
                    ins = None
                    for part in range(2):
                        for k in range(16):
                            c = part * 16 + k
                            col = part * D + k * 128
                            ins = pe.transpose(ptm[:, c * 2:(c + 1) * 2], modrow[0:2, col:col + 128],
                                               identf[0:2, 0:2])
                    return ins
                T.op('pe', tr, reads=[('modrow', c) for c in range(8)] + [('identf',)], writes=[('ptm',)])
                T.op('dve', lambda e: e.tensor_copy(out=shT[:], in_=ptm[:, 0:32].rearrange("p (k r) -> p k r", r=2)),
                     reads=[('ptm',)], writes=[('shT',)])
                T.op('dve', lambda e: e.tensor_scalar(out=tmpT[:], in0=ptm[:, 32:64].rearrange("p (k r) -> p k r", r=2),
                                                      scalar1=1.0, scalar2=None, op0=ALU.add),
                     reads=[('ptm',)], writes=[('tmpT',)])
                T.op('dve', lambda e: e.tensor_tensor(out=gsT[:], in0=tmpT[:],
                                                      in1=pregT[:].unsqueeze(2).to_broadcast([128, 16, 2]),
                                                      op=ALU.mult),
                     reads=[('tmpT',), ('pregT',)], writes=[('gsT',)])
            T.retire()

        def mod_norm_phase(li):
            with contextlib.ExitStack() as ph:
                wb = [sb("mwb%d" % i, [128, 16, 512], BF16, ph) for i in range(2)]
                bm = [sb("mbm%d" % i, [2, 512], F32, ph) for i in range(2)]
                grow = [sb("grow%d" % i, [2, 512], F32, ph) for i in range(2)]
                pregT = sb("pregT", [128, 16], F32, ph)
                tmpT = sb("tmpT", [128, 16, 2], F32, ph)
                pm = [ps("mpm%d" % i, [128, 512], F32, ph) for i in range(2)]
                ptm = ps("ptm", [128, 512], F32, ph)
                xh = [sb("xh%d" % i, [128, D], BF16, ph) for i in range(2)]
                junk = sb("junk", [128, D], BF16, ph)
                stt = sb("nstat", [128, 10, 4], F32, ph)
                pt = [ps("pt%d" % i, [128, 1024], BF16, ph) for i in range(4)]
                NG = 12
                T.dma('sp', pregT[:], pre_g[li:li + 1, :].rearrange("o (k p) -> p (o k)", p=128),
                      writes=[('pregT',)], allow_slow_non_contiguous=True)

                def load(g):
                    b = g % 2
                    cs = slice(g * 512, (g + 1) * 512)
                    T.dma('pool', wb[b][:], w_mod[li, :, cs].rearrange("(k p) c -> p k c", p=128), writes=[('mwb', b)])
                    T.dma('sp', bm[b][:], _bcast_rows(b_mod[li:li + 1, cs], 2), writes=[('mbm', b)])

                def mstep(g):
                    b = g % 2
                    mm_group(pm[b][0:2, :], [(scT[:, k, :], wb[b][:, k, :]) for k in range(16)],
                             reads=[('mwb', b), ('scT',)], writes=[('mpm', b)])
                    T.op('dve', lambda e: e.tensor_tensor(out=grow[b][0:2, :], in0=pm[b][0:2, :], in1=bm[b][0:2, :], op=ALU.add),
                         reads=[('mpm', b), ('mbm', b)], writes=[('grow', b)])
                    if g < 8:
                        part = g // 4

                        def tr(pe):
                            ins = None
                            for kk in range(4):
                                c = part * 16 + (g % 4) * 4 + kk
                                ins = pe.transpose(ptm[:, c * 2:(c + 1) * 2], grow[b][0:2, kk * 128:(kk + 1) * 128],
                                                   identf[0:2, 0:2])
                            return ins
                        T.op('pe', tr, reads=[('grow', b), ('identf',)], writes=[('ptm', g)])
                    else:
                        T.dma('sp', gate_d[0:2, (g - 8) * 512:(g - 7) * 512], grow[b][0:2, :], reads=[('grow', b)],
                              writes=[('gate_d', g - 8)])
                    if g + 2 < NG:
                        load(g + 2)
                    if g == 7:
                        T.op('dve', lambda e: e.tensor_copy(out=shT[:], in_=ptm[:, 0:32].rearrange("p (k r) -> p k r", r=2)),
                             reads=[('ptm', c) for c in range(8)], writes=[('shT',)])
                        T.op('dve', lambda e: e.tensor_scalar(out=tmpT[:], in0=ptm[:, 32:64].rearrange("p (k r) -> p k r", r=2),
                                                              scalar1=1.0, scalar2=None, op0=ALU.add),
                             reads=[('ptm', c) for c in range(8)], writes=[('tmpT',)])
                        T.op('dve', lambda e: e.tensor_tensor(out=gsT[:], in0=tmpT[:],
                                                              in1=pregT[:].unsqueeze(2).to_broadcast([128, 16, 2]), op=ALU.mult),
                             reads=[('tmpT',), ('pregT',)], writes=[('gsT',)])
                load(0)
                load(1)
                msteps = [(lambda g=g: mstep(g)) for g in range(NG)]
                per_tile = [2, 1, 1, 1, 1, 1, 2, 1, 1, 1]
                for tt in range(10):
                    for _ in range(per_tile[tt]):
                        msteps.pop(0)()
                    hv = hview(tt)
                    xb_ = xh[tt % 2]
                    T.op('act', lambda e: e.activation(out=junk[:], in_=hv, func=AF.Square, accum_out=stt[:, tt, 0:1]),
                         reads=[('h', tt)], writes=[('nstat', tt, 0)])
                    T.op('act', lambda e: e.activation(out=stt[:, tt, 1:2], in_=stt[:, tt, 0:1], func=AF.Sqrt,
                                                       bias=ceps[:, 0:1], scale=1.0 / D),
                         reads=[('nstat', tt, 0), ('ceps',)], writes=[('nstat', tt, 1)])
                    T.op('dve', lambda e: e.reciprocal(out=stt[:, tt, 2:3], in_=stt[:, tt, 1:2]),
                         reads=[('nstat', tt, 1)], writes=[('nstat', tt, 2)])
                    T.op('dve', lambda e: e.tensor_scalar(out=xb_[:], in0=hv, scalar1=stt[:, tt, 2:3], scalar2=None,
                                                          op0=ALU.mult),
                         reads=[('h', tt), ('nstat', tt, 2)], writes=[('xh', tt % 2)])
                    for g in range(4):
                        ptv = pt[g][:, 0:512].rearrange("p (a b) -> p a b", b=128)

                        def trf(pe):
                            ins = None
                            for kk in range(4):
                                k = g * 4 + kk
                                ins = pe.transpose(ptv[:, kk, :], xb_[:, k * 128:(k + 1) * 128], identb[:])
                            return ins
                        T.op('pe', trf, reads=[('xh', tt % 2), ('identb',)], writes=[('pt', g)])
                        dst = nT[:, g * 4:(g + 1) * 4, tt * 128:(tt + 1) * 128]
                        wr = [('nT', tt, g * 4 + kk) for kk in range(4)]
                        if g % 2 == 0:
                            T.op('act', lambda e: e.activation(out=dst, in_=ptv, func=AF.Copy), reads=[('pt', g)], writes=wr)
                        else:
                            T.op('dve', lambda e: e.tensor_copy(out=dst, in_=ptv), reads=[('pt', g)], writes=wr)
                while msteps:
                    msteps.pop(0)()
                for k in range(16):
                    lat = nT[:, k, 0:NLAT]
                    T.op('dve', lambda e: e.tensor_scalar(out=lat, in0=lat, scalar1=gsT[:, k, 0:1], scalar2=shT[:, k, 0:1],
                                                          op0=ALU.mult, op1=ALU.add),
                         reads=[('gsT',), ('shT',)] + [('nT', tt, k) for tt in range(8)],
                         writes=[('nT', tt, k) for tt in range(8)])
                    cx = nT[:, k, NLAT:NTOK]
                    T.op('pool', lambda e: e.tensor_scalar(out=cx, in0=cx, scalar1=gsT[:, k, 1:2], scalar2=shT[:, k, 1:2],
                                                           op0=ALU.mult, op1=ALU.add),
                         reads=[('gsT',), ('shT',)] + [('nT', tt, k) for tt in (8, 9)],
                         writes=[('nT', tt, k) for tt in (8, 9)])
            T.retire()

        CGS = [
            [('T', 0, 512, 'Q', 0, 'rope_s')],
            [('T', 0, 512, 'K', 0, 'rope_s')],
            [('R', 0, 512, 'V', 0, None)],
            [('T', 0, 512, 'Q', 4, 'rope_h')],
            [('T', 0, 256, 'K', 4, 'rope_h'), ('R', 256, 256, 'V', 512, None)],
            [('T', 0, 512, 'Q', 8, 'plain')],
            [('T', 0, 512, 'K', 6, 'plain')],
            [('R', 0, 512, 'V', 768, None)],
            [('T', 0, 512, 'Q', 12, 'norm_q')],
            [('T', 0, 256, 'K', 10, 'norm_k'), ('R', 256, 256, 'V', 1280, None)],
            [('T', 0, 512, 'G', 0, 'silu')],
            [('T', 0, 512, 'G', 4, 'silu')],
            [('T', 0, 512, 'G', 8, 'silu')],
            [('T', 0, 512, 'G', 12, 'silu')],
        ]

        def exchange():
            for pi, kb_ in enumerate([(0, 4), (6, 10)]):
                T.allgather(KTl_d[pi], KTa_d[pi],
                            reads=[('KTl_d', b_, tp) for b_ in kb_ for tp in range(4)],
                            writes=[('KTa_d', pi)])
            for pi, vb_ in enumerate([(0, 512), (768, 1280)]):
                T.allgather(Vl_d[pi], Va_d[pi],
                            reads=[('Vl_d', b_, tp) for b_ in vb_ for tp in range(4)],
                            writes=[('Va_d', pi)])

        CG_ORDER = [1, 2, 4, 6, 7, 9, 0, 3, 5, 8, 10, 11, 12, 13]

        def proj_load_w(li, wb, ci_):
            cg_ = CG_ORDER[ci_]
            T.dma('pool', wb[ci_ % 2][:], w_in[li, :, cg_ * 512:(cg_ + 1) * 512].rearrange("(k p) c -> p k c", p=128),
                  writes=[('pwb', ci_ % 2)])

        def proj_phase(li, wb, with_ctx):
            with contextlib.ExitStack() as ph:
                xs = [sb("xs%d" % i, [128, 512], F32, ph) for i in range(2)]
                tu = [sb("tu%d" % i, [128, 512], F32, ph) for i in range(2)]
                NXB, NSTG, LAG = 6, 4, 3
                xb = [sb("xb%d" % i, [128, 512], BF16, ph) for i in range(NXB)]
                stgT = [sb("stgT%d" % i, [128, 4, 256], BF16, ph) for i in range(NSTG)]
                stgR = [sb("stgR%d" % i, [128, 2, 512], BF16, ph) for i in range(3)]
                gqb = sb("gqb", [128, 128], F32, ph)
                gkb = sb("gkb", [128, 128], F32, ph)
                rope = sb("rope", [128, 8, 384], F32, ph)
                nst = sb("pnst", [128, 8, 4], F32, ph)
                pj = [ps("pj%d" % i, [128, 512], F32, ph) for i in range(3)]
                ptq = [ps("ptq%d" % i, [128, 1024], BF16, ph) for i in range(2)]
                T.dma('sp', gqb[:], _bcast_rows(gq[li:li + 1, :], 128), writes=[('gqb',)])
                T.dma('sp', gkb[:], _bcast_rows(gk[li:li + 1, :], 128), writes=[('gkb',)])
                for j in range(8):
                    T.dma('sp', rope[:, j, :], rope_t[j * 128:(j + 1) * 128, :], writes=[('rope', j)])
                cnt = {'pj': 0, 'xs': 0, 'xb': 0, 'ptq': 0, 'T': 0, 'R': 0, 'nst': 0, 'n': 0}
                dq = []

                def run_deferred(now, lag):
                    while dq and dq[0][0] + lag <= now:
                        dq.pop(0)[1]()

                def emit_transposes(item):
                    (xbi, nh, sT, tt) = item
                    pi = cnt['ptq'] % 2
                    cnt['ptq'] += 1
                    pv_ = ptq[pi][:, 0:512].rearrange("p (a b) -> p a b", b=128)

                    def trf(pe):
                        ins = None
                        for hh in range(nh):
                            ins = pe.transpose(pv_[:, hh, :], xb[xbi][:, hh * 128:(hh + 1) * 128], identb[:])
                        return ins
                    T.op('pe', trf, reads=[('xb', xbi), ('identb',)], writes=[('ptq', pi)])
                    T.op('act', lambda e: e.activation(out=stgT[sT][:, 0:nh, (tt % 2) * 128:(tt % 2) * 128 + 128],
                                                       in_=pv_[:, 0:nh, :], func=AF.Copy),
                         reads=[('ptq', pi)], writes=[('stgT', sT, tt % 2)])

                def flush_T(seg, sT, tp):
                    (_, c0, W, dest, base, proc) = seg
                    nh = W // 128
                    src = stgT[sT][:, 0:nh, :]
                    rd = [('stgT', sT, 0), ('stgT', sT, 1)]
                    if dest == 'Q' or dest == 'G':
                        dd = QT_d if dest == 'Q' else GT_d
                        key = ('QT_d' if dest == 'Q' else 'GT_d', base, tp)
                        T.dma('sp', dd[base * 128:(base + nh) * 128, tp * 256:(tp + 1) * 256]
                              .rearrange("(hh d) t -> d hh t", d=128), src, reads=rd, writes=[key])
                    else:
                        if tp < 4:
                            T.dma('sp', KTl_d[base // 6][(base % 6) * 128:(base % 6 + nh) * 128, tp * 256:(tp + 1) * 256]
                                  .rearrange("(hh d) t -> d hh t", d=128), src, reads=rd,
                                  writes=[('KTl_d', base, tp)])
                        else:
                            T.dma('sp', KTc_d[base * 128:(base + nh) * 128, :]
                                  .rearrange("(hh d) t -> d hh t", d=128), src, reads=rd,
                                  writes=[('KTc_d', base)])

                def flush_R(seg, sR, tp):
                    (_, c0, W, dest, base, proc) = seg
                    src = stgR[sR][:, :, 0:W]
                    rd = [('stgR', sR, 0), ('stgR', sR, 1)]
                    if tp < 4:
                        T.dma('sp', Vl_d[base // 768][tp * 256:(tp + 1) * 256, base % 768:base % 768 + W]
                              .rearrange("(i p) c -> p i c", p=128),
                              src, reads=rd, writes=[('Vl_d', base, tp)])
                    else:
                        T.dma('sp', Vc_d[:, base:base + W].rearrange("(i p) c -> p i c", p=128),
                              src, reads=rd, writes=[('Vc_d', base)])

                for ci, cg in enumerate(CG_ORDER):
                    b = ci % 2
                    if ci + 1 < len(CG_ORDER):
                        proj_load_w(li, wb, ci + 1)
                    segs = CGS[cg]
                    cur = {}
                    n_tt = 8 if (not with_ctx and segs[0][3] in ('Q', 'G')) else 10
                    for tt in range(n_tt):
                        lat = tt < 8
                        tp = tt // 2
                        pji = cnt['pj'] % 3
                        cnt['pj'] += 1
                        mm_group(pj[pji][:], [(nT[:, k, tt * 128:(tt + 1) * 128], wb[b][:, k, :]) for k in range(16)],
                                 reads=[('pwb', b)] + [('nT', tt, k) for k in range(16)], writes=[('pj', pji)])
                        cnt['n'] += 1
                        now = cnt['n']
                        run_deferred(now, LAG)
                        for si, seg in enumerate(segs):
                            (kind, c0, W, dest, base, proc) = seg
                            src = pj[pji][:, c0:c0 + W]
                            if kind == 'R':
                                if tt % 2 == 0:
                                    cur[si] = cnt['R'] % 3
                                    cnt['R'] += 1
                                sR = cur[si]
                                T.op('act', lambda e: e.activation(out=stgR[sR][:, tt % 2, 0:W], in_=src, func=AF.Copy),
                                     reads=[('pj', pji)], writes=[('stgR', sR, tt % 2)])
                                if tt % 2 == 1:
                                    flush_R(seg, sR, tp)
                                continue
                            nh = W // 128
                            if tt % 2 == 0:
                                cur[si] = cnt['T'] % 3
                                cnt['T'] += 1
                            sT = cur[si]
                            xbi = cnt['xb'] % NXB
                            cnt['xb'] += 1
                            xbv = xb[xbi][:, 0:W]
                            need_rope = lat and proc in ('rope_s', 'rope_h', 'norm_q', 'norm_k')
                            need_norm = proc in ('norm_q', 'norm_k')
                            if proc == 'silu':
                                T.op('act', lambda e: e.activation(out=xbv, in_=src, func=AF.Silu),
                                     reads=[('pj', pji)], writes=[('xb', xbi)])
                            elif not need_rope and not need_norm:
                                T.op('act', lambda e: e.activation(out=xbv, in_=src, func=AF.Copy),
                                     reads=[('pj', pji)], writes=[('xb', xbi)])
                            else:
                                xi = cnt['xs'] % 2
                                cnt['xs'] += 1
                                xsv = xs[xi][:, 0:W]
                                tuv = tu[xi][:, 0:W]
                                T.op('act', lambda e: e.activation(out=xsv, in_=src, func=AF.Copy),
                                     reads=[('pj', pji)], writes=[('xs', xi)])
                                if need_norm:
                                    ni = cnt['nst'] % 8
                                    cnt['nst'] += 1
                                    gb = gqb if proc == 'norm_q' else gkb
                                    x3 = xsv.rearrange("p (a b) -> p a b", b=128)
                                    t3 = tuv.rearrange("p (a b) -> p a b", b=128)
                                    for hh in range(nh):
                                        T.op('act', lambda e: e.activation(out=t3[:, hh, :], in_=src[:, hh * 128:(hh + 1) * 128],
                                                                           func=AF.Square, accum_out=nst[:, ni, hh:hh + 1]),
                                             reads=[('pj', pji)], writes=[('pnst', ni), ('tu', xi)])
                                    T.op('act', lambda e: e.activation(out=nst[:, ni, 0:nh], in_=nst[:, ni, 0:nh], func=AF.Sqrt,
                                                                       bias=ceps[:, 0:1], scale=1.0 / 128),
                                         reads=[('pnst', ni), ('ceps',)], writes=[('pnst', ni)])
                                    T.op('dve', lambda e: e.reciprocal(out=nst[:, ni, 0:nh], in_=nst[:, ni, 0:nh]),
                                         reads=[('pnst', ni)], writes=[('pnst', ni)])
                                    T.op('dve', lambda e: e.tensor_tensor(out=x3, in0=x3,
                                                                          in1=nst[:, ni, 0:nh].unsqueeze(2).to_broadcast([128, nh, 128]),
                                                                          op=ALU.mult),
                                         reads=[('xs', xi), ('pnst', ni)], writes=[('xs', xi)])
                                    if need_rope:
                                        T.op('dve', lambda e: e.tensor_tensor(out=x3, in0=x3,
                                                                              in1=gb[:].unsqueeze(1).to_broadcast([128, nh, 128]),
                                                                              op=ALU.mult),
                                             reads=[('xs', xi), ('gqb',), ('gkb',)], writes=[('xs', xi)])
                                    else:
                                        T.op('dve', lambda e: e.tensor_tensor(out=xbv.rearrange("p (a b) -> p a b", b=128), in0=x3,
                                                                              in1=gb[:].unsqueeze(1).to_broadcast([128, nh, 128]),
                                                                              op=ALU.mult),
                                             reads=[('xs', xi), ('gqb',), ('gkb',)], writes=[('xb', xbi)])
                                if need_rope:
                                    if proc == 'rope_s':
                                        hd, hf, co = 64, 32, 256
                                    else:
                                        hd, hf, co = 128, 64, 0
                                    nsub = W // hd
                                    x4 = xsv.rearrange("p (a t f) -> p a t f", t=2, f=hf)
                                    u4 = tuv.rearrange("p (a t f) -> p a t f", t=2, f=hf)
                                    x3r = xsv.rearrange("p (a b) -> p a b", b=hd)
                                    Ct = rope[:, tt, co:co + hd].unsqueeze(1).to_broadcast([128, nsub, hd])
                                    Sn = rope[:, tt, co + hd:co + hd + hf].unsqueeze(1).to_broadcast([128, nsub, hf])
                                    Sp = rope[:, tt, co + hd + hf:co + 2 * hd].unsqueeze(1).to_broadcast([128, nsub, hf])
                                    T.op('dve', lambda e: e.tensor_tensor(out=u4[:, :, 0, :], in0=x4[:, :, 1, :], in1=Sn, op=ALU.mult),
                                         reads=[('xs', xi), ('rope', tt)], writes=[('tu', xi)])
                                    T.op('dve', lambda e: e.tensor_tensor(out=u4[:, :, 1, :], in0=x4[:, :, 0, :], in1=Sp, op=ALU.mult),
                                         reads=[('xs', xi), ('rope', tt)], writes=[('tu', xi)])
                                    T.op('dve', lambda e: e.tensor_tensor(out=x3r, in0=x3r, in1=Ct, op=ALU.mult),
                                         reads=[('xs', xi), ('rope', tt)], writes=[('xs', xi)])
                                    T.op('dve', lambda e: e.tensor_tensor(out=xbv, in0=xsv, in1=tuv, op=ALU.add),
                                         reads=[('xs', xi), ('tu', xi)], writes=[('xb', xbi)])
                            dq.append((now, (lambda it=(xbi, nh, sT, tt): emit_transposes(it))))
                            if tt % 2 == 1:
                                dq.append((now, (lambda seg=seg, sT=sT, tp=tp: flush_T(seg, sT, tp))))
                    if cg == 9:
                        dq.append((cnt['n'], exchange))
                run_deferred(10 ** 9, 0)
            T.retire()

        def attn_phase(li, l_abs, with_ctx):
            lam_init = 0.8 - 0.6 * math.exp(-0.3 * l_abs)
            with contextlib.ExitStack() as ph:
                QTh = [sb("QTh%d" % i, [128, NTOK], BF16, ph) for i in range(2)]
                GTh = [sb("GTh%d" % i, [128, NTOK], BF16, ph) for i in range(2)]
                QZ1 = [sb("QZ%d" % i, [128, NTOK], BF16, ph) for i in range(2)]
                QZ = [[QTh[i], QZ1[i]] for i in range(2)]
                KTh = [sb("KTh%d" % i, [128, 3328], BF16, ph) for i in range(2)]
                Vh = [sb("Vh%d" % i, [128, 26, 129], BF16, ph) for i in range(2)]
                Ef = [sb("Ef%d" % i, [128, 512], BF16, ph) for i in range(3)]
                El = [sb("El%d" % i, [128, 8, 128], BF16, ph) for i in range(2)]
                tmpS = [sb("tmpS%d" % i, [128, 6, 128], F32, ph) for i in range(2)]
                tabC = [sb("tabC%d" % i, [128, 6, 128], F32, ph) for i in range(2)]
                btb = sb("btb", [128, 6, 128], F32, ph)
                stash = sb("stash", [128, 10, 128], F32, ph)
                onb4 = [sb("onb%d" % i, [128, 4, 128], BF16, ph) for i in range(3)]
                a4 = [sb("af%d" % i, [128, 4, 128], F32, ph) for i in range(3)]
                af = a4
                sq4 = tmpS[0]
                pvsb = [sb("pvsb%d" % i, [128, 4, 132], F32, ph) for i in range(3)]
                sm = sb("asm", [128, 16, 16], F32, ph)
                lp = {}
                lamt = a4[2][:, 0:2, :].rearrange("p a b -> p (a b)")
                lsc = sb("lsc", [128, 8], F32, ph)
                subg = sb("subg", [128, 128], F32, ph)
                sinkt = sb("sinkt", [128, 4], F32, ph)
                esink = sb("esink", [128, 4], F32, ph)
                PS = ps("PS", [128, 6, 512], F32, ph)
                ptos = [ps("pto%d" % i, [128, 1024], BF16, ph) for i in range(2)]
                cnt = {'sf': 0, 'ef': 0, 'sl': 0, 'pvl': 0, 'onb': 0, 'sm': 0, 'pto': 0, 'af': 0, 'tab': 0, 'q': 0, 'kv': 0, 'pvsb': 0}

                T.dma('sp', lamt, _bcast_rows(lamv[li:li + 1, :], 128), writes=[('lamt',), ('a4', 2)])
                T.dma('sp', subg[:], _bcast_rows(subln_g[li:li + 1, :], 128), writes=[('subg',)])
                T.dma('sp', sinkt[:], _bcast_rows(sink[li:li + 1, :], 128), writes=[('sinkt',)])
                T.dma('sp', btb[:], btab.rearrange("n k q -> k n q"), writes=[('btb',)])
                for i in range(2):
                    T.op('dve', lambda e: e.memset(Vh[i][:, :, 128:129], 1.0), writes=[('Vh', i)])
                    for p in range(2):
                        T.op('pool', lambda e: e.memset(QZ[i][p][:], 0.0), writes=[('QZ', i), ('QTh', i)])
                    T.op('dve', lambda e: e.scalar_tensor_tensor(out=a4[0][:, 0, 0:64],
                                                                 in0=lamt[:, i * 128:i * 128 + 64], scalar=1.0,
                                                                 in1=lamt[:, i * 128 + 64:i * 128 + 128],
                                                                 op0=ALU.mult, op1=ALU.mult,
                                                                 accum_out=lsc[:, i:i + 1]),
                         reads=[('lamt',)], writes=[('lsc', i), ('a4', 0)])
                T.op('act', lambda e: e.activation(out=lsc[:, 2:4], in_=lsc[:, 0:2], func=AF.Exp),
                     reads=[('lsc', 0), ('lsc', 1)], writes=[('lsc', 2)])
                T.op('act', lambda e: e.activation(out=esink[:], in_=sinkt[:], func=AF.Exp),
                     reads=[('sinkt',)], writes=[('esink',)])
                T.op('dve', lambda e: e.tensor_tensor(out=lsc[:, 4:5], in0=lsc[:, 2:3], in1=lsc[:, 3:4], op=ALU.subtract),
                     reads=[('lsc', 2)], writes=[('lsc', 4)])
                T.op('dve', lambda e: e.tensor_scalar(out=lsc[:, 5:6], in0=lsc[:, 4:5], scalar1=lam_init, scalar2=-1.0,
                                                      op0=ALU.add, op1=ALU.mult),
                     reads=[('lsc', 4)], writes=[('nlam',)])
                T.op('dve', lambda e: e.tensor_scalar(out=subg[:], in0=subg[:], scalar1=(1.0 - lam_init), scalar2=None,
                                                      op0=ALU.mult),
                     reads=[('subg',)], writes=[('subg',)])

                def load_kv(m, g, vbase, local):
                    bi = cnt['kv'] % 2
                    cnt['kv'] += 1
                    kt, vt = KTh[bi], Vh[bi]
                    pi = g // 6
                    gl = g % 6
                    lrows = slice(gl * 128, (gl + 1) * 128)
                    rows = slice(g * 128, (g + 1) * 128)
                    vcols = slice(vbase, vbase + 128)
                    vlc = slice(vbase % 768, vbase % 768 + 128)
                    kl_reads = [('KTl_d', b_, tp) for b_ in (0, 4, 6, 10) for tp in range(4)]
                    vl_reads = [('Vl_d', b_, tp) for b_ in (0, 512, 768, 1280) for tp in range(4)]
                    if not local:
                        KTa3 = KTa_d[pi].rearrange("(r x) t -> x r t", r=2)
                        T.dma('sp', kt[:, 0:2048].rearrange("p (r t) -> p r t", r=2), KTa3[lrows, :, :],
                              reads=[('KTa_d', pi)], writes=[('KTh', bi)])
                        T.dma('sp', vt[:, 0:16, 0:128], Va_d[pi][:, vlc].rearrange("(j p) c -> p j c", p=128),
                              reads=[('Va_d', pi)], writes=[('Vh', bi)])
                    else:
                        T.dma('sp', kt[:, 768:1024], KTa_d[pi][gl * 128:(gl + 1) * 128, 768:1024],
                              reads=[('KTa_d', pi)], writes=[('KTh', bi)])
                        T.dma('sp', kt[:, 1024:1280], KTa_d[pi][768 + gl * 128:768 + (gl + 1) * 128, 0:256],
                              reads=[('KTa_d', pi)], writes=[('KTh', bi)])
                        T.dma('sp', kt[:, 2304:3328], KTl_d[pi][lrows, :],
                              reads=kl_reads, writes=[('KTh', bi)])
                        T.dma('sp', vt[:, 6:10, 0:128], Va_d[pi][768:1280, vlc].rearrange("(j p) c -> p j c", p=128),
                              reads=[('Va_d', pi)], writes=[('Vh', bi)])
                        T.dma('sp', vt[:, 18:26, 0:128], Vl_d[pi][:, vlc].rearrange("(j p) c -> p j c", p=128),
                              reads=vl_reads, writes=[('Vh', bi)])
                    T.dma('sp', kt[:, 2048:2304], KTc_d[rows, :],
                          reads=[('KTc_d', b_) for b_ in (0, 4, 6, 10)], writes=[('KTh', bi)])
                    T.dma('sp', vt[:, 16:18, 0:128], Vc_d[:, vcols].rearrange("(j p) c -> p j c", p=128),
                          reads=[('Vc_d', b_) for b_ in (0, 512, 768, 1280)], writes=[('Vh', bi)])
                    return bi

                def load_q(chunk, split=False):
                    bi = cnt['q'] % 2
                    cnt['q'] += 1
                    rows = slice(chunk * 128, (chunk + 1) * 128)
                    qreads = [('QT_d', b_, tp) for b_ in (0, 4, 8, 12) for tp in range(5)]
                    if split:
                        for p in range(2):
                            T.dma('sp', QZ[bi][p][64 * p:64 * p + 64, :], QT_d[chunk * 128 + 64 * p:chunk * 128 + 64 * p + 64, :],
                                  reads=qreads, writes=[('QZ', bi), ('QTh', bi)])
                    else:
                        T.dma('sp', QTh[bi][:], QT_d[rows, :], reads=qreads, writes=[('QTh', bi), ('QZ', bi)])
                    T.dma('sp', GTh[bi][:], GT_d[rows, :],
                          reads=[('GT_d', b_, tp) for b_ in (0, 4, 8, 12) for tp in range(5)], writes=[('GTh', bi)])
                    return bi

                def new_sm():
                    i = cnt['sm'] % 16
                    cnt['sm'] += 1
                    return i

                def new_pvsb():
                    i = cnt['pvsb'] % 3
                    cnt['pvsb'] += 1
                    return i

                def post_from_sbuf(kind, chunk, qb, q0, nq, pi, p=None, sink_ap=None):
                    pv = pvsb[pi]
                    pk = [('pvsb', pi)]
                    si = new_sm()
                    rd = sm[:, si, 0:nq]
                    if sink_ap is not None:
                        T.op('dve', lambda e: e.tensor_scalar(out=rd, in0=pv[:, 0:nq, 128], scalar1=sink_ap, scalar2=None,
                                                              op0=ALU.add),
                             reads=pk + [('esink',)], writes=[('asm', si)])
                        T.op('dve', lambda e: e.reciprocal(out=rd, in_=rd), reads=[('asm', si)], writes=[('asm', si)])
                    else:
                        T.op('dve', lambda e: e.reciprocal(out=rd, in_=pv[:, 0:nq, 128]), reads=pk, writes=[('asm', si)])
                    rd_b = rd.unsqueeze(2).to_broadcast([128, nq, 128])
                    pvv = pv[:, 0:nq, 0:128]
                    if kind == 'A' and p == 0:
                        T.op('dve', lambda e: e.tensor_tensor(out=stash[:, q0:q0 + nq, :], in0=pvv, in1=rd_b, op=ALU.mult),
                             reads=pk + [('asm', si)], writes=[('stash', q0 + qi) for qi in range(nq)])
                        return
                    oi = cnt['onb'] % 3
                    cnt['onb'] += 1
                    ob = onb4[oi][:, 0:nq, :]
                    part1b = None
                    if kind == 'A':
                        ai = cnt['af'] % 3
                        cnt['af'] += 1
                        a = a4[ai][:, 0:nq, :]
                        r1 = sm[:, si, 4:4 + nq]
                        ss = sm[:, si, 8:8 + nq]
                        T.op('dve', lambda e: e.tensor_scalar(out=r1, in0=rd, scalar1=lsc[:, 5:6], scalar2=None, op0=ALU.mult),
                             reads=[('asm', si), ('nlam',)], writes=[('asm', si)])
                        T.op('dve', lambda e: e.tensor_tensor(out=a, in0=pvv, in1=r1.unsqueeze(2).to_broadcast([128, nq, 128]),
                                                              op=ALU.mult),
                             reads=pk + [('asm', si)], writes=[('a4', ai)])
                        T.op('dve', lambda e: e.tensor_tensor(out=a, in0=a, in1=stash[:, q0:q0 + nq, :], op=ALU.add),
                             reads=[('a4', ai)] + [('stash', q0 + qi) for qi in range(nq)], writes=[('a4', ai)])
                        T.op('dve', lambda e: e.tensor_tensor(out=sq4[:, 0:nq, :], in0=a, in1=a, op=ALU.mult),
                             reads=[('a4', ai)], writes=[('tmpS', 0)])
                        T.op('dve', lambda e: e.tensor_reduce(out=ss, in_=sq4[:, 0:nq, :], axis=AX.X, op=ALU.add),
                             reads=[('tmpS', 0)], writes=[('asm', si)])

                        def part1b():
                            T.op('act', lambda e: e.activation(out=ss, in_=ss, func=AF.Ln, bias=ceps[:, 0:1], scale=1.0 / 128),
                                 reads=[('asm', si), ('ceps',)], writes=[('asm', si)])
                            T.op('act', lambda e: e.activation(out=ss, in_=ss, func=AF.Exp, scale=-0.5),
                                 reads=[('asm', si)], writes=[('asm', si)])
                            T.op('dve', lambda e: e.tensor_tensor(out=a, in0=a,
                                                                  in1=subg[:].unsqueeze(1).to_broadcast([128, nq, 128]),
                                                                  op=ALU.mult),
                                 reads=[('a4', ai), ('subg',)], writes=[('a4', ai)])
                            T.op('dve', lambda e: e.tensor_tensor(out=ob, in0=a,
                                                                  in1=ss.unsqueeze(2).to_broadcast([128, nq, 128]),
                                                                  op=ALU.mult),
                                 reads=[('a4', ai), ('asm', si)], writes=[('onb', oi)])
                    else:
                        T.op('dve', lambda e: e.tensor_tensor(out=ob, in0=pvv, in1=rd_b, op=ALU.mult),
                             reads=pk + [('asm', si)], writes=[('onb', oi)])
                    def part2():
                        if part1b is not None:
                            part1b()
                        ti = cnt['pto'] % 2
                        cnt['pto'] += 1
                        pt_ = ptos[ti][:, 0:nq * 128]

                        def trf(pe):
                            ins = None
                            for qi in range(nq):
                                ins = pe.transpose(ptos[ti][:, qi * 128:(qi + 1) * 128], onb4[oi][:, qi, :], identb[:])
                            return ins
                        T.op('pe', trf, reads=[('onb', oi), ('identb',)], writes=[('pto', ti)])
                        T.op('dve', lambda e: e.tensor_tensor(out=nT[:, chunk, q0 * 128:(q0 + nq) * 128], in0=pt_,
                                                              in1=GTh[qb][:, q0 * 128:(q0 + nq) * 128], op=ALU.mult),
                             reads=[('pto', ti), ('GTh', qb)], writes=[('nT', q0 + qi, chunk) for qi in range(nq)])
                    lp['defer'].append([lp['t'] + (14 if kind == 'A' else 8), part2])

                NSB = 4
                LOOK = 3

                def full_iters(hb, q0, kblocks, ksl, scale, post, batch4, qz=None):
                    nq = 2
                    N = 256
                    npair = len(kblocks) // 2
                    its = []
                    for i in range(npair):
                        it = {'look': LOOK, 'type': 'F'}

                        def S(it=it, i=i):
                            qb, kb = hb['qb'], hb['kb']
                            qsrc = QTh[qb] if qz is None else QZ[qb][qz]
                            qkey = ('QTh', qb) if qz is None else ('QZ', qb)
                            sbk = cnt['sf'] % NSB
                            cnt['sf'] += 1
                            it['sbk'] = sbk

                            def sf(pe):
                                ins = None
                                for hh in range(2):
                                    kcol, vt = kblocks[2 * i + hh]
                                    ins = pe.matmul(PS[:, sbk, hh * 256:hh * 256 + N], KTh[kb][ksl, kcol:kcol + 128],
                                                    qsrc[ksl, q0 * 128:q0 * 128 + N], start=True, stop=True)
                                return ins
                            T.op('pe', sf, reads=[('KTh', kb), qkey], writes=[('PS', sbk)])

                        def EP(it=it, i=i):
                            qb, kb = hb['qb'], hb['kb']
                            sbk = it['sbk']
                            ei = cnt['ef'] % 3
                            cnt['ef'] += 1
                            T.op('act', lambda e: e.activation(out=Ef[ei][:, 0:512], in_=PS[:, sbk, 0:512], func=AF.Exp, scale=scale),
                                 reads=[('PS', sbk)], writes=[('Ef', ei)])

                            def pvf(pe):
                                ins = None
                                for hh in range(2):
                                    vt = kblocks[2 * i + hh][1]
                                    for qi in range(nq):
                                        ins = pe.matmul(PS[:, 4 + qi, 0:129],
                                                        Ef[ei][:, hh * 256 + qi * 128:hh * 256 + (qi + 1) * 128],
                                                        Vh[kb][:, vt, :], start=(i == 0 and hh == 0),
                                                        stop=(i == npair - 1 and hh == 1))
                                return ins
                            T.op('pe', pvf, reads=[('Ef', ei), ('Vh', kb)], writes=[('PS', 4), ('PS', 5)])
                            if i != npair - 1:
                                return
                            if batch4:
                                slot = q0 % 4
                                if slot == 0:
                                    lp['fpi'] = new_pvsb()
                                pi = lp['fpi']
                            else:
                                slot = 0
                                pi = new_pvsb()
                            if batch4:
                                T.op('dve', lambda e: e.tensor_copy(out=pvsb[pi][:, slot:slot + 2, 0:129], in_=PS[:, 4:6, 0:129]),
                                     reads=[('PS', 4), ('PS', 5)], writes=[('pvsb', pi)])
                            else:
                                T.op('act', lambda e: e.activation(out=pvsb[pi][:, slot:slot + 2, 0:129], in_=PS[:, 4:6, 0:129],
                                                                   func=AF.Copy),
                                     reads=[('PS', 4), ('PS', 5)], writes=[('pvsb', pi)])
                            if batch4:
                                if slot == 2:
                                    post(hb['qb'], q0 - 2, 4, pi)
                            else:
                                post(hb['qb'], q0, 2, pi)
                        it['S'] = S
                        it['EP'] = EP
                        its.append(it)
                    return its

                def local_iters(hb, specs, scale, post):
                    its = []
                    for j in range(8):
                        it = {'look': 1, 'type': 'L'}

                        def S(it=it, j=j):
                            qb, kb = hb['qb'], hb['kb']
                            tabled, tab_ap, tab_reads, plain = specs[j]()
                            blocks = tabled + plain
                            nbt = len(tabled)
                            nb = len(blocks)
                            s_ = cnt['sl'] % 2
                            cnt['sl'] += 1
                            it['s'] = s_
                            it['blocks'] = blocks
                            Sv = PS[:, 2 * s_:2 * s_ + 2, :].rearrange("p a (b c) -> p (a b) c", c=128)

                            def sf(pe):
                                ins = None
                                for i, (kcol, vt) in enumerate(blocks):
                                    ins = pe.matmul(Sv[:, i, :], KTh[kb][:, kcol:kcol + 128], QTh[qb][:, j * 128:(j + 1) * 128],
                                                    start=True, stop=True)
                                return ins
                            T.op('pe', sf, reads=[('KTh', kb), ('QTh', qb)], writes=[('PS', 2 * s_), ('PS', 2 * s_ + 1)])
                            el = El[s_]
                            T.op('dve', lambda e: e.scalar_tensor_tensor(out=tmpS[s_][:, 0:nbt, :], in0=Sv[:, 0:nbt, :], scalar=scale,
                                                                         in1=tab_ap, op0=ALU.mult, op1=ALU.add),
                                 reads=[('PS', 2 * s_), ('PS', 2 * s_ + 1)] + tab_reads, writes=[('tmpS', s_)])
                            T.op('act', lambda e: e.activation(out=el[:, 0:nbt, :], in_=tmpS[s_][:, 0:nbt, :], func=AF.Exp),
                                 reads=[('tmpS', s_)], writes=[('El', s_, 0)])
                            T.op('act', lambda e: e.activation(out=el[:, nbt:nb, :], in_=Sv[:, nbt:nb, :], func=AF.Exp, scale=scale),
                                 reads=[('PS', 2 * s_), ('PS', 2 * s_ + 1)], writes=[('El', s_, 1)])

                        def EP(it=it, j=j):
                            kb = hb['kb']
                            s_, blocks = it['s'], it['blocks']
                            nb = len(blocks)
                            el = El[s_]
                            pvi = 4 + (cnt['pvl'] % 2)
                            cnt['pvl'] += 1

                            def pvf(pe):
                                ins = None
                                for i, (kcol, vt) in enumerate(blocks):
                                    ins = pe.matmul(PS[:, pvi, 0:129], el[:, i, :], Vh[kb][:, vt, :], start=(i == 0),
                                                    stop=(i == nb - 1))
                                return ins
                            T.op('pe', pvf, reads=[('El', s_, 0), ('El', s_, 1), ('Vh', kb)], writes=[('PS', pvi)])
                            if j % 4 == 0:
                                lp['pi'] = new_pvsb()
                            pi = lp['pi']
                            T.op('act', lambda e: e.activation(out=pvsb[pi][:, j % 4, 0:129], in_=PS[:, pvi, 0:129], func=AF.Copy),
                                 reads=[('PS', pvi)], writes=[('pvsb', pi)])
                            if j % 4 == 3:
                                post(hb['qb'], j - 3, 4, pi)
                        it['S'] = S
                        it['EP'] = EP
                        its.append(it)
                    return its

                def run_stream(its):
                    n = len(its)
                    nxt = 0
                    lp['defer'] = []
                    for t in range(n):
                        lp['t'] = t
                        while lp['defer'] and lp['defer'][0][0] <= t:
                            lp['defer'].pop(0)[1]()
                        while nxt < n:
                            look = its[nxt]['look']
                            if nxt > 0 and its[nxt]['type'] != its[nxt - 1]['type']:
                                look = 0
                            if nxt > t + look:
                                break
                            its[nxt]['S']()
                            nxt += 1
                        if 'PRE' in its[t]:
                            while lp['defer']:
                                lp['defer'].pop(0)[1]()
                            its[t]['PRE']()
                        its[t]['EP']()
                    while lp['defer']:
                        lp['defer'].pop(0)[1]()

                ALLK = [(a * 128, a) for a in range(16)] + [(2048 + c * 128, 16 + c) for c in range(2)]
                CTXK = [(2048 + c * 128, 16 + c) for c in range(2)]
                A128 = slice(0, 128)

                def own(jj):
                    return (2304 + jj * 128, 18 + jj)

                def allk(a):
                    return (a * 128, a)

                sc = 128 ** -0.5
                stream = []
                heads = []

                for hq in range(4):
                    hb = {}
                    its = []
                    for p in range(2):
                        pst = (lambda qb, q0, nq, pi, p=p, hq=hq: post_from_sbuf('A', hq, qb, q0, nq, pi, p=p))
                        for g in range(4):
                            its += full_iters(hb, 2 * g, ALLK, A128, 0.125, pst, True, qz=p)
                        if with_ctx:
                            its += full_iters(hb, 8, CTXK, A128, 0.125, pst, False, qz=p)
                    heads.append((hb, (lambda hq=hq: (load_kv(0, hq, hq * 128, False), load_q(hq, split=True))), its))
                for kvh in range(2):
                    for hq in (2 * kvh, 2 * kvh + 1):
                        hb = {}
                        pst = (lambda qb, q0, nq, pi, hq=hq: post_from_sbuf('G', 4 + hq, qb, q0, nq, pi,
                                                                             sink_ap=esink[:, hq:hq + 1]))

                        def bspec(j):
                            if j == 0:
                                return ([allk(7), own(1)], btb[:, 2:4, :], [('btb',)], [own(j)] + CTXK)
                            if j == 7:
                                return ([own(6), allk(8)], btb[:, 4:6, :], [('btb',)], [own(j)] + CTXK)
                            return ([own(j - 1), own(j + 1)], btb[:, 0:2, :], [('btb',)], [own(j)] + CTXK)
                        its = local_iters(hb, [(lambda j=j: bspec(j)) for j in range(8)], sc, pst)
                        if with_ctx:
                            its += full_iters(hb, 8, CTXK, A128, sc, pst, False)
                        if hq == 2 * kvh:
                            ld = (lambda kvh=kvh, hq=hq: (load_kv(1, 4 + kvh, 512 + kvh * 128, True), load_q(4 + hq)))
                        else:
                            ld = (lambda hq=hq: (None, load_q(4 + hq)))
                        heads.append((hb, ld, its))
                for hq in range(4):
                    hb = {}
                    pst = (lambda qb, q0, nq, pi, hq=hq: post_from_sbuf('G', 8 + hq, qb, q0, nq, pi))

                    def cspec(j, hq=hq):
                        dl = C_DELTAS[j]
                        tabled = []
                        for dlt in dl:
                            jj = j + dlt
                            if jj < 0:
                                tabled.append(allk(8 + jj))
                            elif jj > 7:
                                tabled.append(allk(jj))
                            else:
                                tabled.append(own(jj))
                        ti = cnt['tab'] % 2
                        cnt['tab'] += 1
                        n = len(dl)
                        T.dma('sp', tabC[ti][:, 0:n, :], ctab[li, hq, C_TAB0[j]:C_TAB0[j] + n].rearrange("n k q -> k n q"),
                              writes=[('tabC', ti)])
                        return (tabled, tabC[ti][:, 0:n, :], [('tabC', ti)], CTXK)
                    its = local_iters(hb, [(lambda j=j, cspec=cspec: cspec(j)) for j in range(8)], sc, pst)
                    if with_ctx:
                        its += full_iters(hb, 8, CTXK, A128, sc, pst, False)
                    heads.append((hb, (lambda hq=hq: (load_kv(2, 6 + hq, 768 + hq * 128, True), load_q(8 + hq))), its))
                for kvh in range(2):
                    for hq in (2 * kvh, 2 * kvh + 1):
                        hb = {}
                        pst = (lambda qb, q0, nq, pi, hq=hq: post_from_sbuf('G', 12 + hq, qb, q0, nq, pi))
                        its = []
                        for g in range(4):
                            its += full_iters(hb, 2 * g, ALLK, A128, sc, pst, True)
                        if with_ctx:
                            its += full_iters(hb, 8, CTXK, A128, sc, pst, False)
                        if hq == 2 * kvh:
                            ld = (lambda kvh=kvh, hq=hq: (load_kv(3, 10 + kvh, 1280 + kvh * 128, False), load_q(12 + hq)))
                        else:
                            ld = (lambda hq=hq: (None, load_q(12 + hq)))
                        heads.append((hb, ld, its))

                state = {'kb': None}

                def do_load(hi):
                    hb, ld, _ = heads[hi]
                    kbn, qbn = ld()
                    if kbn is not None:
                        state['kb'] = kbn
                    hb['kb'] = state['kb']
                    hb['qb'] = qbn
                for hi, (hb, ld, its) in enumerate(heads):
                    def pre(hi=hi):
                        if hi + 1 < len(heads):
                            do_load(hi + 1)
                    its[16 if len(its) >= 32 else 0]['PRE'] = pre
                    stream += its
                do_load(0)
                run_stream(stream)
            T.retire()

        def out_phase(li, with_ctx):
            ntile = 10 if with_ctx else 8
            with contextlib.ExitStack() as ph:
                wo = sb("wo", [128, 16, D], BF16, ph)
                gp = sb("gp", [128, D], F32, ph)
                pgb = sb("pgb", [128, D], F32, ph)
                st2 = sb("ost", [128, 10, 8], F32, ph)
                py = [ps("py%d" % i, [128, 4, 512], F32, ph) for i in range(2)]
                for cg in range(4):
                    T.dma('pool', wo[:, :, cg * 512:(cg + 1) * 512],
                          w_out[li, :, cg * 512:(cg + 1) * 512].rearrange("(k p) c -> p k c", p=128),
                          writes=[('wo', cg)])
                T.dma('sp', pgb[:], _bcast_rows(post_g[li:li + 1, :], 128), writes=[('pgb',)])

                def make_gp(r):
                    T.dma('sp', gp[:], _bcast_rows(gate_d[r:r + 1, :], 128), reads=[('gate_d', g_) for g_ in range(4)], writes=[('gp',)])
                    T.op('dve', lambda e: e.tensor_tensor(out=gp[:], in0=gp[:], in1=pgb[:], op=ALU.mult),
                         reads=[('gp',), ('pgb',)], writes=[('gp',)])

                make_gp(0)
                for tt in range(ntile):
                    if tt == 8:
                        make_gp(1)
                    pi = tt % 2
                    pyv = py[pi]
                    for cg in range(4):
                        mm_group(pyv[:, cg, :], [(nT[:, k, tt * 128:(tt + 1) * 128], wo[:, k, cg * 512:(cg + 1) * 512])
                                                 for k in range(16)],
                                 reads=[('wo', cg)] + [('nT', tt, k) for k in range(16)], writes=[('py', pi, cg)])
                    for cg in range(4):
                        T.op('act', lambda e: e.activation(out=pgb_junk[:], in_=pyv[:, cg, :],
                                                           func=AF.Square, accum_out=st2[:, tt, cg:cg + 1]),
                             reads=[('py', pi, cg)], writes=[('ost', tt, cg)])
                    T.op('dve', lambda e: e.tensor_reduce(out=st2[:, tt, 4:5], in_=st2[:, tt, 0:4], axis=AX.X, op=ALU.add),
                         reads=[('ost', tt, c) for c in range(4)], writes=[('ost', tt, 4)])
                    T.op('act', lambda e: e.activation(out=st2[:, tt, 5:6], in_=st2[:, tt, 4:5], func=AF.Sqrt,
                                                       bias=ceps[:, 0:1], scale=1.0 / D),
                         reads=[('ost', tt, 4), ('ceps',)], writes=[('ost', tt, 5)])
                    T.op('dve', lambda e: e.reciprocal(out=st2[:, tt, 6:7], in_=st2[:, tt, 5:6]),
                         reads=[('ost', tt, 5)], writes=[('ost', tt, 6)])
                    for cg in range(4):
                        T.op('dve', lambda e: e.tensor_tensor(out=pyv[:, cg, :], in0=pyv[:, cg, :],
                                                              in1=gp[:, cg * 512:(cg + 1) * 512], op=ALU.mult),
                             reads=[('py', pi, cg), ('gp',), ('ost', tt, cg)], writes=[('py', pi, cg)])
                    hv = hview(tt)
                    T.op('dve', lambda e: e.scalar_tensor_tensor(out=hv, in0=pyv[:].rearrange("p a b -> p (a b)"),
                                                                 scalar=st2[:, tt, 6:7], in1=hv, op0=ALU.mult, op1=ALU.add),
                         reads=[('py', pi, c) for c in range(4)] + [('ost', tt, 6), ('h', tt)], writes=[('h', tt)])
            T.retire()

        pgb_junk = sb("sqjunk", [128, 512], BF16)

        import os as _os
        _stop = _os.environ.get("K_STOP", "")
        for li, l_abs in enumerate(layer_ids):
            with_ctx = l_abs < 3
            if _stop == "setup":
                break
            if _stop == "mod":
                break
            with contextlib.ExitStack() as wsc:
                wbuf = [sb("pwb%d" % i, [128, 16, 512], BF16, wsc) for i in range(2)]
                T.persist.add('pwb')
                proj_load_w(li, wbuf, 0)
                mod_norm_phase(li)
                T.persist.discard('pwb')
                if _stop == "norm":
                    break
                proj_phase(li, wbuf, with_ctx)
            if _stop == "proj":
                break
            if _stop == "exch":
                break
            attn_phase(li, l_abs, with_ctx)
            if _stop == "attn":
                break
            out_phase(li, with_ctx)

        for j in range(8):
            T.dma('sp', out_sh[j * 128:(j + 1) * 128, :], h[:, j, :], reads=[('h', j)], writes=[('out', j)])
        if write_ctx:
            for j in range(2):
                T.dma('sp', ctx_out[j * 128:(j + 1) * 128, :], hc[:, j, :], reads=[('h', 8 + j)], writes=[('out', 8 + j)])
        T.finish('sp')
        build.trk = T
        build.stats = dict(cnt={k: v for k, v in T.cnt.items() if not isinstance(k, tuple)}, nwait=T.nwait)
    return nc


def _rope_tables():
    def tab(dim):
        t = np.arange(2048)
        row = (t // GRID_W).astype(np.float32)
        col = (t % GRID_W).astype(np.float32)
        nf = dim // 4
        inv = (np.float32(10000.0) ** (-np.arange(nf, dtype=np.float32) / nf)).astype(np.float32)
        ang = np.concatenate([row[:, None] * inv, col[:, None] * inv], axis=-1).astype(np.float32)
        return np.cos(ang).astype(np.float32), np.sin(ang).astype(np.float32)
    ch, sh_ = tab(128)
    cs, ss = tab(64)
    return np.concatenate([ch, ch, -sh_, sh_, cs, cs, -ss, ss], axis=1).astype(np.float32)


def _c_index(half):
    kp = np.arange(128)[:, None]
    qp = np.arange(128)[None, :]
    rr = np.zeros((NCT, 128, 128), np.int64)
    cc = np.zeros((NCT, 128, 128), np.int64)
    ok = np.zeros((NCT, 128, 128), bool)
    done = set()
    for j in range(8):
        for i, dlt in enumerate(C_DELTAS[j]):
            tid = C_TAB0[j] + i
            if tid in done:
                continue
            done.add(tid)
            jj = j + dlt
            if (jj < 0 and half == 0) or (jj > 7 and half == 1):
                continue
            tq = 8 * half + j
            tk = tq + dlt
            if tk < 0 or tk > 15:
                continue
            kr = 2 * tk + kp // 64
            kc = kp % 64
            r = 2 * tq + qp // 64
            c = qp % 64
            r0 = np.clip(r - 4, 0, 24)
            c0 = np.clip(c - 8, 0, 48)
            valid = (kr >= r0) & (kr < r0 + 8) & (kc >= c0) & (kc < c0 + 16)
            rr[tid] = np.clip(kr - r + 7, 0, 14)
            cc[tid] = np.clip(kc - c + 15, 0, 30)
            ok[tid] = valid
    return rr, cc, ok


def _b_tables(half):
    kp = np.arange(128)[:, None]
    qp = np.arange(128)[None, :]
    negs = np.full((128, 128), NEG, np.float32)
    tri_prev = np.where(qp <= kp, np.float32(0), np.float32(NEG)).astype(np.float32)
    tri_next = np.where(kp <= qp, np.float32(0), np.float32(NEG)).astype(np.float32)
    x_prev = tri_prev if half == 1 else negs
    x_next = tri_next if half == 0 else negs
    return np.stack([tri_prev, tri_next, x_prev, tri_next, tri_prev, x_next]).astype(np.float32)


_PROG_CACHE = {}


def _get_prog(layer_ids, write_ctx):
    key = (tuple(layer_ids), write_ctx)
    if key not in _PROG_CACHE:
        _PROG_CACHE[key] = build(list(layer_ids), write_ctx)
    return _PROG_CACHE[key]


def _in_maps(x_parts, ctx_parts, c, c_ctx, lsl, w_mod, b_mod, pre_norm_g, w_in, lam4, diff_subln_g, win_sink,
             na_rpb, qk_q_norm_g, qk_k_norm_g, w_out, post_norm_g):
    rope = _rope_tables()
    ident_b = np.eye(128, dtype=np.float32).astype(ml_dtypes.bfloat16)
    ident_f = np.eye(128, dtype=np.float32)
    maps = []
    ctab_half = []
    for half in range(2):
        rr, cc, ok = _c_index(half)
        rp = na_rpb[lsl]
        g = rp[:, :, rr, cc]
        ctab_half.append(np.where(ok[None, None], g, np.float32(NEG)).astype(np.float32))
    btabs = [_b_tables(0), _b_tables(1)]
    for core in range(8):
        b, half = core // 2, core % 2
        maps.append({
            "x_sh": np.ascontiguousarray(x_parts[core]),
            "ctx_b": np.ascontiguousarray(ctx_parts[b]),
            "cvec": np.ascontiguousarray(np.stack([c[b], c_ctx]).astype(np.float32)),
            "w_mod": w_mod[lsl], "b_mod": b_mod[lsl], "pre_g": pre_norm_g[lsl], "post_g": post_norm_g[lsl],
            "w_in": w_in[lsl], "w_out": w_out[lsl],
            "lamv": np.ascontiguousarray(lam4[lsl].reshape(-1, 256)),
            "subln_g": diff_subln_g[lsl], "sink": win_sink[lsl], "gq": qk_q_norm_g[lsl], "gk": qk_k_norm_g[lsl],
            "rope_t": np.ascontiguousarray(rope[half * 1024:(half + 1) * 1024]),
            "ident_b": ident_b, "ident_f": ident_f,
            "ctab": ctab_half[half], "btab": btabs[half],
            "sel_in": np.concatenate([np.repeat(np.eye(2, dtype=np.float32)[:, 0:1], 128, 1),
                                      np.repeat(np.eye(2, dtype=np.float32)[:, 1:2], 128, 1)], axis=1),
        })
    return maps


FUSED = True


def kernel(x, c, ctx, c_ctx, w_mod, b_mod, pre_norm_g, w_in, diff_lambda_q1, diff_lambda_k1,
           diff_lambda_q2, diff_lambda_k2, diff_subln_g, win_sink, na_rpb, qk_q_norm_g, qk_k_norm_g,
           w_out, post_norm_g):
    f = lambda a: np.ascontiguousarray(np.asarray(a, dtype=np.float32))
    x, c, ctx, c_ctx = f(x), f(c), f(ctx), f(c_ctx)
    w_mod, b_mod, pre_norm_g, w_in, w_out, post_norm_g = f(w_mod), f(b_mod), f(pre_norm_g), f(w_in), f(w_out), f(post_norm_g)
    diff_subln_g, win_sink, na_rpb, qk_q_norm_g, qk_k_norm_g = f(diff_subln_g), f(win_sink), f(na_rpb), f(qk_q_norm_g), f(qk_k_norm_g)
    lam4 = np.stack([f(diff_lambda_q1), f(diff_lambda_k1), f(diff_lambda_q2), f(diff_lambda_k2)], axis=1)
    x_parts = [x[core // 2, (core % 2) * 1024:(core % 2 + 1) * 1024] for core in range(8)]
    ctx_parts = [ctx[b] for b in range(4)]
    groups = [[0, 1, 2, 3]] if FUSED else [[0], [1], [2], [3]]
    for gi, lids in enumerate(groups):
        last = gi == len(groups) - 1
        lsl = slice(lids[0], lids[-1] + 1)
        nc = _get_prog(lids, not last)
        maps = _in_maps(x_parts, ctx_parts, c, c_ctx, lsl, w_mod, b_mod, pre_norm_g, w_in, lam4, diff_subln_g,
                        win_sink, na_rpb, qk_q_norm_g, qk_k_norm_g, w_out, post_norm_g)
        res = run_bass_kernel_spmd(nc, maps, core_ids=list(range(8)))
        x_parts = [np.asarray(res.results[core]["out_sh"]) for core in range(8)]
        if not last:
            ctx_parts = [np.asarray(res.results[2 * b]["ctx_out"]) for b in range(4)]
    out = np.empty((4, 2048, 2048), np.float32)
    for core in range(8):
        out[core // 2, (core % 2) * 1024:(core % 2 + 1) * 1024] = x_parts[core]
    return out
```
